# Optimizing a Trainium2 kernel written in Bass

```python
import jax, jax.numpy as jnp
from jax import lax
import numpy as np

D_MODEL = 1024
BATCH = 16
SEQ = 2048
DEPTH = 4

CHUNK = 64
D_CONV = 512
CONV_HEADS = 8
D_POOL = 512
POOL_WINDOWS = (2, 4, 8, 16)
N_POOL = len(POOL_WINDOWS)
POOL_GROUP = D_POOL // N_POOL
D_MIX = D_CONV + D_POOL
D_IN = 2 * D_CONV + D_POOL
CONV_K = 31
D_FF = 2816
FFN_CONV_K = 3
N_MOD = 6
EPS = 1e-6

kernel_name = "hybrid_conformer_pool_streaming_trunk"


def rms_norm(x, g):
    xf = x.astype(jnp.float32)
    y = xf * lax.rsqrt(jnp.mean(xf * xf, axis=-1, keepdims=True) + EPS)
    return (y * g.astype(jnp.float32)).astype(x.dtype)


def layer_norm(x, g, b):
    xf = x.astype(jnp.float32)
    mu = jnp.mean(xf, axis=-1, keepdims=True)
    xc = xf - mu
    var = jnp.mean(xc * xc, axis=-1, keepdims=True)
    y = xc * lax.rsqrt(var + EPS)
    return (y * g.astype(jnp.float32) + b.astype(jnp.float32)).astype(x.dtype)


def causal_dwconv(x, w, b):
    k = w.shape[0]
    ch = x.shape[-1]
    y = lax.conv_general_dilated(
        x, w[:, None, :].astype(x.dtype), window_strides=(1,), padding=[(k - 1, 0)],
        dimension_numbers=("NWC", "WIO", "NWC"), feature_group_count=ch)
    return y + b.astype(x.dtype)


def conformer_conv_mixer(u_val, u_gate, conv_w, conv_b, ln_g, ln_b):
    a = u_val * jax.nn.sigmoid(u_gate)
    a = causal_dwconv(a, conv_w, conv_b)
    a = layer_norm(a, ln_g, ln_b)
    return jax.nn.silu(a)


def multiscale_pool_mixer(h, pool_w, pool_scale):
    s = h.shape[1]
    hf = h.astype(jnp.float32)
    cs = lax.cumsum(hf, axis=1)
    t = jnp.arange(s)
    outs = []
    for g, w in enumerate(POOL_WINDOWS):
        sl = slice(g * POOL_GROUP, (g + 1) * POOL_GROUP)
        csg = cs[..., sl]
        prev = jnp.pad(csg, ((0, 0), (w, 0), (0, 0)))[:, :s]
        cnt = jnp.minimum(t + 1, w).astype(jnp.float32)[None, :, None]
        d = ((csg - prev) / cnt - hf[..., sl]).astype(h.dtype)
        outs.append(jnp.einsum("bsc,cd->bsd", d, pool_w[g]))
    return jnp.concatenate(outs, axis=-1) * pool_scale


def setup_inputs(seed: int = 0) -> dict:
    key = jax.random.key(seed)
    ks = jax.random.split(key, 24)
    f32 = jnp.float32

    def nrm(k, shape, scale):
        return jax.random.normal(k, shape, f32) * scale

    L = DEPTH
    return {
        "x": nrm(ks[0], (BATCH, SEQ, D_MODEL), 1.0),
        "c": nrm(ks[1], (BATCH, D_MODEL), 1.0),
        "ada_w": nrm(ks[2], (L, D_MODEL, N_MOD * D_MODEL), 0.1 * D_MODEL ** -0.5),
        "ada_b": nrm(ks[3], (L, N_MOD * D_MODEL), 0.01),
        "pre_mix_g": 1.0 + nrm(ks[4], (L, D_MODEL), 0.05),
        "post_mix_g": 1.0 + nrm(ks[5], (L, D_MODEL), 0.05),
        "w_in": nrm(ks[6], (L, D_MODEL, D_IN), D_MODEL ** -0.5),
        "conv_w": nrm(ks[7], (L, CONV_K, D_CONV), CONV_K ** -0.5),
        "conv_b": nrm(ks[8], (L, D_CONV), 0.01),
        "conv_ln_g": 1.0 + nrm(ks[9], (L, D_CONV), 0.05),
        "conv_ln_b": nrm(ks[10], (L, D_CONV), 0.01),
        "pool_w": nrm(ks[11], (L, N_POOL, POOL_GROUP, POOL_GROUP), POOL_GROUP ** -0.5),
        "pool_scale": 1.0 + nrm(ks[12], (L, D_POOL), 0.1),
        "w_out": nrm(ks[13], (L, D_MIX, D_MODEL), D_MIX ** -0.5),
        "pre_ffn_g": 1.0 + nrm(ks[14], (L, D_MODEL), 0.05),
        "post_ffn_g": 1.0 + nrm(ks[15], (L, D_MODEL), 0.05),
        "ffn_up": nrm(ks[16], (L, D_MODEL, 2 * D_FF), D_MODEL ** -0.5),
        "ffn_conv_w": nrm(ks[17], (L, FFN_CONV_K, 2 * D_FF), FFN_CONV_K ** -0.5),
        "ffn_conv_b": nrm(ks[18], (L, 2 * D_FF), 0.01),
        "ffn_down": nrm(ks[19], (L, D_FF, D_MODEL), D_FF ** -0.5),
    }


def reference(x, c, ada_w, ada_b, pre_mix_g, post_mix_g, w_in, conv_w, conv_b, conv_ln_g,
              conv_ln_b, pool_w, pool_scale, w_out, pre_ffn_g, post_ffn_g, ffn_up,
              ffn_conv_w, ffn_conv_b, ffn_down):
    c_act = jax.nn.silu(c)
    for l in range(DEPTH):
        mod = c_act @ ada_w[l] + ada_b[l]
        sh1, sc1, gt1, sh2, sc2, gt2 = [m[:, None, :] for m in jnp.split(mod, N_MOD, axis=-1)]

        h = rms_norm(x, pre_mix_g[l]) * (1.0 + sc1) + sh1
        u = jnp.einsum("bsd,de->bse", h, w_in[l])
        a = conformer_conv_mixer(u[..., :D_CONV], u[..., D_CONV:2 * D_CONV],
                                 conv_w[l], conv_b[l], conv_ln_g[l], conv_ln_b[l])
        p = multiscale_pool_mixer(u[..., 2 * D_CONV:], pool_w[l], pool_scale[l])
        o = jnp.einsum("bsm,md->bsd", jnp.concatenate([a, p], axis=-1), w_out[l])
        x = x + (1.0 + gt1) * rms_norm(o, post_mix_g[l])

        h = rms_norm(x, pre_ffn_g[l]) * (1.0 + sc2) + sh2
        u = jnp.einsum("bsd,df->bsf", h, ffn_up[l])
        u = causal_dwconv(u, ffn_conv_w[l], ffn_conv_b[l])
        hid = jax.nn.silu(u[..., D_FF:]) * u[..., :D_FF]
        o = jnp.einsum("bsf,fd->bsd", hid, ffn_down[l])
        x = x + (1.0 + gt2) * rms_norm(o, post_ffn_g[l])
    return x
```

```python
import contextlib
import numpy as np
import concourse.bass as bass
import concourse.mybir as mybir
from concourse.bass_utils import run_bass_kernel_spmd

F32 = mybir.dt.float32
BF16 = mybir.dt.bfloat16
AF = mybir.ActivationFunctionType
ALU = mybir.AluOpType

D = 1024
DC = 512
DFF = 2816
NF = 22
TT = 512
EPS = 1e-6
NSLOT = 10
POOL_W = (2, 4, 8, 16)


class Buf:
    def __init__(self, name, t, psum=False):
        self.name = name
        self.t = t
        self.psum = psum
        self.writer = None
        self.readers = []
        self.dsem = None
        self.dcnt = 0

    def __getitem__(self, k):
        return self.t[k]


class Em:
    def __init__(self, nc, stack, needed=None):
        self.nc = nc
        self.stack = stack
        self.needed = needed
        self.waited = set()
        self.real = {}
        self.realmap = {}
        self.eng = {"pe": nc.tensor, "act": nc.scalar, "dve": nc.vector, "pool": nc.gpsimd, "sp": nc.sync}
        self.sem = {k: stack.enter_context(nc.semaphore("sem_" + k)) for k in self.eng}
        self.cnt = {k: 0 for k in self.eng}
        self.real = {k: 0 for k in self.eng}
        self.seen = {k: {} for k in self.eng}
        self.nwaits = 0
        self.ninst = 0

    def sb(self, name, shape, dt):
        return Buf(name, self.nc.alloc_sbuf_tensor(name, list(shape), dt))

    def ps(self, name, shape, dt=F32):
        return Buf(name, self.nc.alloc_psum_tensor(name, list(shape), dt), psum=True)

    def _deps(self, engine, reads, writes):
        deps = []
        for b in reads:
            if b.writer is not None:
                deps.append((b.writer, True))
            if b.psum:
                deps += [(r, True) for r in b.readers]
        for b in writes:
            if b.writer is not None:
                deps.append((b.writer, b.psum))
            deps += [(r, b.psum) for r in b.readers]
        need = {}
        for (key, sem, val, src), raw in deps:
            if src == engine:
                if engine == "pe" or engine == "sp":
                    continue
                if not raw or val < self.cnt[engine]:
                    continue
            if key not in need or need[key][1] < val:
                need[key] = (sem, val)
        return need

    def _wait(self, engine, need):
        e = self.eng[engine]
        seen = self.seen[engine]
        for key, (sem, val) in need.items():
            if seen.get(key, 0) >= val:
                continue
            rv = val
            if key in self.eng:
                self.waited.add((key, val))
                if self.needed is not None:
                    rv = self.realmap[(key, val)]
            e.wait_ge(sem, rv)
            seen[key] = val
            self.nwaits += 1

    def _mark(self, dep, reads, writes):
        for b in reads:
            if b.psum:
                b.writer = dep
                b.readers = []
            else:
                b.readers = [r for r in b.readers if r[0] != dep[0]] + [dep]
        for b in writes:
            b.writer = dep
            b.readers = []

    def _inc(self, engine, inst):
        self.cnt[engine] += 1
        if self.needed is None or (engine, self.cnt[engine]) in self.needed:
            inst.then_inc(self.sem[engine], 1)
            self.real[engine] += 1
            self.realmap[(engine, self.cnt[engine])] = self.real[engine]

    def op(self, engine, fn, reads=(), writes=()):
        need = self._deps(engine, reads, writes)
        self._wait(engine, need)
        inst = fn(self.eng[engine])
        self._inc(engine, inst)
        self.ninst += 1
        dep = (engine, self.sem[engine], self.cnt[engine], engine)
        self._mark(dep, reads, writes)

    def multi(self, engine, fns, reads=(), writes=()):
        need = self._deps(engine, reads, writes)
        self._wait(engine, need)
        inst = None
        for fn in fns:
            inst = fn(self.eng[engine])
            self.ninst += 1
        self._inc(engine, inst)
        dep = (engine, self.sem[engine], self.cnt[engine], engine)
        self._mark(dep, reads, writes)

    def mm_group(self, out_buf, out_ap, pairs, reads, first=True, last=True):
        need = self._deps("pe", reads, [out_buf])
        self._wait("pe", need)
        n = len(pairs)
        inst = None
        for i, (l, r) in enumerate(pairs):
            inst = self.nc.tensor.matmul(out_ap, l, r, start=(first and i == 0), stop=(last and i == n - 1))
            self.ninst += 1
        self._inc("pe", inst)
        dep = ("pe", self.sem["pe"], self.cnt["pe"], "pe")
        self._mark(dep, reads, [out_buf])

    def dma(self, queue, out_ap, in_ap, reads, writes, owner):
        if owner.dsem is None:
            owner.dsem = self.stack.enter_context(self.nc.semaphore("dsem_" + owner.name))
        need = self._deps(queue, reads, writes)
        if owner.dcnt > 0:
            key = ("d", owner.name)
            if key not in need or need[key][1] < owner.dcnt:
                need[key] = (owner.dsem, owner.dcnt)
        self._wait(queue, need)
        inst = self.eng[queue].dma_start(out=out_ap, in_=in_ap)
        owner.dcnt += 16
        inst.then_inc(owner.dsem, 16)
        self.ninst += 1
        dep = (("d", owner.name), owner.dsem, owner.dcnt, None)
        self._mark(dep, reads, writes)
        return dep


def build(L, n_seq, seq_len, dbg=None):
    _, needed = _build(L, n_seq, seq_len, None)
    nc, _ = _build(L, n_seq, seq_len, needed)
    return nc


def _build(L, n_seq, seq_len, needed):
    P = 128
    dbg = None
    nc = bass.Bass("TRN2", target_bir_lowering=False)
    NTOK = n_seq * seq_len
    tiles_per_seq = seq_len // TT
    n_tiles = n_seq * tiles_per_seq

    def din(name, shape):
        return nc.dram_tensor(name, list(shape), F32, kind="ExternalInput").ap()

    x_d = din("x", [NTOK, D])
    c_d = din("c", [n_seq, D])
    ada_w = din("ada_w", [L, D, 6 * D])
    ada_b = din("ada_b", [L, 6 * D])
    pre_mix_g = din("pre_mix_g", [L, D])
    post_mix_g = din("post_mix_g", [L, D])
    w_in = din("w_in", [L, D, 3 * DC])
    conv_w = din("conv_w", [L, 31, DC])
    conv_b = din("conv_b", [L, DC])
    conv_ln_g = din("conv_ln_g", [L, DC])
    conv_ln_b = din("conv_ln_b", [L, DC])
    pool_w = din("pool_w", [L, 4, 128, 128])
    pool_scale = din("pool_scale", [L, DC])
    w_out = din("w_out", [L, D, D])
    pre_ffn_g = din("pre_ffn_g", [L, D])
    post_ffn_g = din("post_ffn_g", [L, D])
    ffn_up = din("ffn_up", [L, D, 2 * DFF])
    ffn_conv_w = din("ffn_conv_w", [L, 3, 2 * DFF])
    ffn_conv_b = din("ffn_conv_b", [L, 2 * DFF])
    ffn_down = din("ffn_down", [L, DFF, D])
    ident_d = din("ident", [P, P])
    tm_d = din("tmats", [12, P, P])
    y_d = nc.dram_tensor("y", [NTOK, D], F32, kind="ExternalOutput").ap()
    dbg_d = None
    if dbg is not None:
        dbg_d = nc.dram_tensor("dbg", [P, dbg], F32, kind="ExternalOutput").ap()

    def dscr(name, shape, dt=BF16):
        return nc.dram_tensor(name, list(shape), dt, kind="Internal").ap()

    s_in = dscr("s_in", [L, D, 3 * DC])
    s_out = dscr("s_out", [L, D, D])
    s_pool = dscr("s_pool", [L, 4, P, P])
    s_up = dscr("s_up", [L, D, 2 * DFF])
    s_down = dscr("s_down", [L, DFF, D])
    modscr = dscr("modscr", [L, 12, D], F32)

    with contextlib.ExitStack() as stack:
        em = Em(nc, stack, needed)

        xs = [em.sb(f"x{s}", [P, D], F32) for s in range(4)]
        hT = [em.sb(f"hT{k}", [P, TT], BF16) for k in range(8)]
        hid = [em.sb(f"hid{j}", [P, TT], BF16) for j in range(NF)]
        ring = [em.sb(f"ring{i}", [P, 4096], BF16) for i in range(NSLOT)]
        poolw = em.sb("poolw", [P, 4, P], BF16)
        xn = [em.sb(f"xn{i}", [P, D], BF16) for i in range(4)]
        import os
        JK = os.environ.get("K_JUNK", "0")
        junk = em.sb("junk", [P, D], BF16) if JK != "0" else None
        aT = [em.sb(f"aT{c}", [P, 30 + TT], BF16) for c in range(4)]
        ND = 32
        diag = [em.sb(f"diag{i}", [P, P], BF16) for i in range(ND)]
        cacc = [em.sb(f"cacc{c}", [P, TT], F32) for c in range(4)]
        upool = [em.sb(f"upool{s}", [P, DC], BF16) for s in range(4)]
        dT, ybf, ysq = hid[8:10], hid[10:12], hid[12:14]
        lnm = em.sb("lnm", [P, TT], F32)
        lnr = em.sb("lnr", [P, TT], F32)
        lnt = em.sb("lnt", [P, TT], F32)
        zt = [em.sb(f"zt{i}", [P, TT], F32) for i in range(2)]
        sig = zt
        mT = hid[0:8]
        ggb = [em.sb(f"ggb{g}", [P, D], F32) for g in range(2)]
        facc = cacc + zt + [lnt, lnr]
        ptmps = [em.sb(f"ptmp{i}", [P, D], F32) for i in range(2)]
        ptmp = ptmps[0]
        st4 = em.sb("st4", [P, 16], F32)
        stat = [em.sb(f"stat{i}", [P, 2], F32) for i in range(4)]
        pst = em.sb("pst", [P, 8], F32)
        fcorr = em.sb("fcorr", [P, 2 * NF, 2], F32)
        fcorr2 = em.sb("fcorr2", [P, 2 * NF], F32)
        ident_f = em.sb("ident_f", [P, P], F32)
        ident_b = em.sb("ident_b", [P, P], BF16)
        ones_b = em.sb("ones_b", [P, P], BF16)
        tm = em.sb("tm", [P, 12, P], BF16)
        stage = em.sb("stage", [P, P], F32)
        cT = em.sb("cT", [P, 16], BF16)
        colT = [em.sb(f"colT{l}", [P, 96], F32) for l in range(L)]
        cwT = [em.sb(f"cwT{l}", [P, 124], F32) for l in range(L)]
        smT = [em.sb(f"smT{l}", [P, 16], F32) for l in range(L)]
        fwT = [em.sb(f"fwT{l}", [P, 132], F32) for l in range(L)]
        fbT = [em.sb(f"fbT{l}", [P, 44], F32) for l in range(L)]
        ahalo = [[em.sb(f"ahalo{l}_{c}", [P, 30], BF16) for c in range(4)] for l in range(L)]
        phalo = [em.sb(f"phalo{l}", [P, DC], BF16) for l in range(L)]
        fhalo = [em.sb(f"fhalo{l}", [P, 2 * NF, 2], F32) for l in range(L)]
        trow, brow, grow = ptmp, ggb[0], ggb[1]
        bank = [em.ps(f"bank{i}", [P, 512], F32) for i in range(8)]
        if needed is not None:
            print("sbuf bytes remaining:", nc.sbuf_bytes_remaining)

        bank_rr = [0]
        dctr = [0]

        def nb():
            b = bank[bank_rr[0] % 8]
            bank_rr[0] += 1
            return b

        cvb = [{w: Buf(f"cv{l}_{w}", None) for w in ("in", "pool", "out", "up", "down")} for l in range(L)]
        modb = [Buf(f"modb{l}", None) for l in range(L)]
        yb = [Buf(f"yb{s}", None) for s in range(4)]

        def convert_weights(l):
            for w, dst, src in (("in", s_in[l], w_in[l]), ("pool", s_pool[l], pool_w[l]), ("out", s_out[l], w_out[l]),
                                ("up", s_up[l], ffn_up[l]), ("down", s_down[l], ffn_down[l])):
                em.dma("pool", dst, src, [], [cvb[l][w]], cvb[l][w])

        em.dma("sp", ident_f[:], ident_d, [], [ident_f], ident_f)
        em.op("dve", lambda e: e.tensor_copy(out=ident_b[:], in_=ident_f[:]), [ident_f], [ident_b])
        em.op("dve", lambda e: e.memset(ones_b[:], 1.0 / 512.0), [], [ones_b])
        for i in range(12):
            em.dma("sp", stage[:], tm_d[i], [], [stage], stage)
            em.op("dve", lambda e, i=i: e.tensor_copy(out=tm[:, i, :], in_=stage[:]), [stage], [tm])

        def transpose_rows(src_ap, nrows, dst_buf, dst_ap, evac_engine="dve", func=None):
            em.dma("sp", stage[0:nrows, :], src_ap, [], [stage], stage)
            b = nb()
            em.op("pe", lambda e: e.transpose(b[:, 0:nrows], stage[0:nrows, :], ident_f[0:nrows, 0:nrows]),
                  [stage, ident_f], [b])
            if func is None:
                em.op("dve", lambda e: e.tensor_copy(out=dst_ap, in_=b[:, 0:nrows]), [b], [dst_buf])
            else:
                em.op("act", lambda e: e.activation(out=dst_ap, in_=b[:, 0:nrows], func=func), [b], [dst_buf])

        transpose_rows(c_d.rearrange("b (k p) -> (b k) p", p=P), n_seq * 8, cT, cT[:, 0:n_seq * 8], func=AF.Silu)

        for l in range(L):
            transpose_rows(conv_w[l].rearrange("k (c p) -> (k c) p", p=P), 124, cwT[l], cwT[l][:, :])
            for i, v in enumerate((conv_b, conv_ln_g, conv_ln_b, pool_scale)):
                transpose_rows(v[l].rearrange("(c p) -> c p", p=P), 4, smT[l], smT[l][:, 4 * i:4 * i + 4])
            fw = ffn_conv_w[l].rearrange("k (j p) -> (k j) p", p=P)
            transpose_rows(fw[0:128], 128, fwT[l], fwT[l][:, 0:128])
            transpose_rows(fw[128:132], 4, fwT[l], fwT[l][:, 128:132])
            transpose_rows(ffn_conv_b[l].rearrange("(j p) -> j p", p=P), 44, fbT[l], fbT[l][:, :])

        slot_i = [0]

        def next_slot():
            s = ring[slot_i[0] % NSLOT]
            slot_i[0] += 1
            return s

        gains = {1: pre_mix_g, 2: post_mix_g, 4: pre_ffn_g, 5: post_ffn_g}
        convert_weights(0)
        for l in range(L):
            for v in range(6):
                halves = []
                for h in range(2):
                    sl = next_slot()
                    c0 = v * D + h * 512
                    em.dma("pool", sl[:, :].rearrange("p (k n) -> p k n", k=8),
                           ada_w[l][:, c0:c0 + 512].rearrange("(k p) n -> p k n", p=P), [], [sl], sl)
                    b = nb()
                    slv = sl[:, :].rearrange("p (k n) -> p k n", k=8)
                    cTv = cT[:, 0:n_seq * 8].rearrange("p (b k) -> p b k", k=8)
                    em.mm_group(b, b[0:n_seq, :], [(cTv[:, :, k], slv[:, k, :]) for k in range(8)], [cT, sl])
                    halves.append(b)
                em.dma("sp", brow[0:n_seq, :], ada_b[l:l + 1, v * D:(v + 1) * D].broadcast_to([n_seq, D]), [], [brow], brow)
                for h in range(2):
                    em.op("dve", lambda e, h=h: e.tensor_tensor(out=trow[0:n_seq, h * 512:(h + 1) * 512],
                                                                in0=halves[h][0:n_seq, :],
                                                                in1=brow[0:n_seq, h * 512:(h + 1) * 512], op=ALU.add),
                          [halves[h], brow], [trow])
                if v in gains:
                    em.dma("sp", grow[0:n_seq, :], gains[v][l:l + 1, :].broadcast_to([n_seq, D]), [], [grow], grow)
                    em.op("dve", lambda e: e.scalar_tensor_tensor(out=trow[0:n_seq, :], in0=trow[0:n_seq, :], scalar=1.0,
                                                                  in1=grow[0:n_seq, :], op0=ALU.add, op1=ALU.mult),
                          [trow, grow], [trow])
                em.dma("sp", modscr[l, 2 * v:2 * v + n_seq, :], trow[0:n_seq, :], [trow], [modb[l]], modb[l])
            em.dma("sp", stage[0:96, :], modscr[l].rearrange("q (c p) -> (q c) p", p=P), [modb[l]], [stage], stage)
            b = nb()
            em.op("pe", lambda e: e.transpose(b[:, 0:96], stage[0:96, :], ident_f[0:96, 0:96]), [stage, ident_f], [b])
            em.op("dve", lambda e, l=l: e.tensor_copy(out=colT[l][:, :], in_=b[:, 0:96]), [b], [colT[l]])


        def v8(sl, n=512):
            return sl[:, 0:8 * n].rearrange("p (k n) -> p k n", k=8)

        def v_down(sl, nj):
            return sl[:, 0:nj * D].rearrange("p (j n) -> p j n", j=nj)

        sched = []
        for i in range(n_tiles):
            for l in range(L):
                for q in range(3):
                    sched.append((cvb[l]["in"], [(lambda sl: v8(sl),
                                       s_in[l][:, q * 512:(q + 1) * 512].rearrange("(k p) n -> p k n", p=P))]))
                for q in range(2):
                    sched.append((cvb[l]["out"], [(lambda sl: v8(sl),
                                       s_out[l][:, q * 512:(q + 1) * 512].rearrange("(k p) n -> p k n", p=P))]))
                for q in range(6):
                    n = 512 if q < 5 else 256
                    for g in range(2):
                        c0 = g * DFF + q * 512
                        sched.append((cvb[l]["up"], [(lambda sl, n=n: v8(sl, n),
                                           s_up[l][:, c0:c0 + n].rearrange("(k p) n -> p k n", p=P))]))
                for q in range(6):
                    nj = 4 if q < 5 else 2
                    sched.append((cvb[l]["down"], [(lambda sl, nj=nj: v_down(sl, nj),
                                       s_down[l][q * 512:q * 512 + nj * P, :].rearrange("(j p) n -> p j n", p=P))]))
        issued = [0]
        consumed = [0]
        released = [0]
        base_slot = slot_i[0]

        def pump():
            while issued[0] < len(sched) and issued[0] < released[0] + NSLOT:
                q = issued[0]
                cv, dmas = sched[q]
                sl = ring[(base_slot + q) % NSLOT]
                for dst_fn, src in dmas:
                    em.dma("sp", dst_fn(sl), src, [cv], [sl], sl)
                issued[0] += 1

        def take():
            p = consumed[0]
            pump()
            assert issued[0] > p, "ring too small for the pieces held at once"
            consumed[0] += 1
            return ring[(base_slot + p) % NSLOT]

        def release(n):
            released[0] += n
            assert released[0] <= consumed[0]
            pump()

        def pre_elem(s):
            xb = xn[s]
            jb = junk if JK in ("1", "3") else xb
            em.op("act", lambda e: e.activation(out=jb[:, :], in_=xs[s][:, :], func=AF.Square, accum_out=pst[:, s:s + 1]),
                  [xs[s]], [jb, pst])
            em.op("act", lambda e: e.activation(out=pst[:, 4 + s:5 + s], in_=pst[:, s:s + 1], func=AF.Sqrt, scale=1.0 / D, bias=EPS),
                  [pst], [pst])
            em.op("dve", lambda e: e.reciprocal(out=pst[:, 4 + s:5 + s], in_=pst[:, 4 + s:5 + s]), [pst], [pst])
            em.op("dve", lambda e: e.tensor_scalar(out=xb[:, :], in0=xs[s][:, :], scalar1=pst[:, 4 + s:5 + s], scalar2=None,
                                                   op0=ALU.mult), [xs[s], pst], [xb])

        def pre_finish(l, seq, vsc, vsh):
            ca = (vsc * 2 + seq) * 8
            cb = (vsh * 2 + seq) * 8
            tb = [nb(), nb(), nb(), nb()]
            for s in range(4):
                xb = xn[s]
                fns = []
                for k in range(8):
                    bv = tb[k // 2][:, :].bitcast(BF16)
                    o = bv[:, (k % 2) * 512 + s * 128:(k % 2) * 512 + (s + 1) * 128]
                    fns.append(lambda e, o=o, xb=xb, k=k: e.transpose(o, xb[:, k * 128:(k + 1) * 128], ident_b[:, :]))
                em.multi("pe", fns, [xb, ident_b], tb)
            for k in range(8):
                b = tb[k // 2]
                bv = b[:, :].bitcast(BF16)
                src = bv[:, (k % 2) * 512:(k % 2) * 512 + 512]
                if k % 2 == 0:
                    em.op("act", lambda e, k=k, src=src: e.activation(out=hT[k][:, :], in_=src, func=AF.Identity,
                                                                     scale=colT[l][:, ca + k:ca + k + 1],
                                                                     bias=colT[l][:, cb + k:cb + k + 1]), [b, colT[l]], [hT[k]])
                else:
                    em.op("dve", lambda e, k=k, src=src: e.tensor_scalar(out=hT[k][:, :], in0=src, scalar1=colT[l][:, ca + k:ca + k + 1],
                                                                        scalar2=colT[l][:, cb + k:cb + k + 1], op0=ALU.mult,
                                                                        op1=ALU.add), [b, colT[l]], [hT[k]])

        def postnorm(vg, s, ob, final=None, pool_add=False):
            st = stat[s]
            g = ggb[0 if vg == 2 else 1]
            for h in range(2):
                jb2 = junk if JK in ("2", "3") else ptmp
                em.op("act", lambda e, h=h: e.activation(out=jb2[:, h * 512:(h + 1) * 512], in_=ob[h][:, :], func=AF.Square,
                                                         accum_out=st[:, h:h + 1]), [ob[h]], [jb2, st])
            em.op("dve", lambda e: e.tensor_tensor(out=st[:, 0:1], in0=st[:, 0:1], in1=st[:, 1:2], op=ALU.add), [st], [st])
            em.op("act", lambda e: e.activation(out=st[:, 1:2], in_=st[:, 0:1], func=AF.Sqrt, scale=1.0 / D, bias=EPS), [st], [st])
            em.op("dve", lambda e: e.reciprocal(out=st[:, 1:2], in_=st[:, 1:2]), [st], [st])
            for h in range(2):
                em.op("dve", lambda e, h=h: e.scalar_tensor_tensor(out=ptmp[:, h * 512:(h + 1) * 512], in0=ob[h][:, :],
                                                                   scalar=st[:, 1:2], in1=g[:, h * 512:(h + 1) * 512],
                                                                   op0=ALU.mult, op1=ALU.mult), [ob[h], st, g], [ptmp])
            if final is None:
                em.op("pool" if pool_add else "dve",
                      lambda e: e.tensor_tensor(out=xs[s][:, :], in0=xs[s][:, :], in1=ptmp[:, :], op=ALU.add),
                      [xs[s], ptmp], [xs[s]])
            else:
                r0, nr0 = final
                em.op("dve", lambda e: e.tensor_tensor(out=ptmp[:, :], in0=xs[s][:, :], in1=ptmp[:, :], op=ALU.add),
                      [xs[s], ptmp], [ptmp])
                em.dma("sp", y_d[r0:r0 + 128, :], ptmp[:, :], [ptmp], [yb[s]], yb[s])
                if nr0 is not None:
                    em.dma("sp", xs[s][:, :], x_d[nr0:nr0 + 128, :], [], [xs[s]], xs[s])

        def load_gg(l, seq):
            for gi, v in enumerate((2, 5)):
                em.dma("sp", ggb[gi][:, :], modscr[l, 2 * v + seq:2 * v + seq + 1, :].broadcast_to([P, D]),
                       [modb[l]], [ggb[gi]], ggb[gi])

        def mixer(i, l):
            seq = i // tiles_per_seq
            first = (i % tiles_per_seq) == 0
            dbase = dctr[0]

            def build_diag(c, j):
                dg = diag[(dbase + c * 31 + j) % ND]
                w = cwT[l][:, j * 4 + c:j * 4 + c + 1]
                if j % 2 == 0:
                    em.op("dve", lambda e: e.tensor_scalar(out=dg[:, :], in0=ident_b[:, :], scalar1=w, scalar2=None, op0=ALU.mult),
                          [ident_b, cwT[l]], [dg])
                else:
                    em.op("act", lambda e: e.activation(out=dg[:, :], in_=ident_b[:, :], func=AF.Copy, scale=w),
                          [ident_b, cwT[l]], [dg])

            pre_finish(l, seq, 1, 0)
            w_val, w_gate, w_pool = take(), take(), take()
            for c in range(4):
                if first:
                    em.op("pool", lambda e, c=c: e.memset(aT[c][:, 0:30], 0.0), [], [aT[c]])
                else:
                    em.op("pool", lambda e, c=c: e.tensor_copy(out=aT[c][:, 0:30], in_=ahalo[l][c][:, :]),
                          [ahalo[l][c]], [aT[c]])
                bv, bg = nb(), nb()
                if c == 0:
                    for k in range(8):
                        em.mm_group(bv, bv[:, :], [(v8(w_val)[:, k, 0:128], hT[k][:, :])], [w_val, hT[k]],
                                    first=(k == 0), last=(k == 7))
                else:
                    em.mm_group(bv, bv[:, :], [(v8(w_val)[:, k, c * 128:(c + 1) * 128], hT[k][:, :]) for k in range(8)],
                                [w_val] + hT)
                em.mm_group(bg, bg[:, :], [(v8(w_gate)[:, k, c * 128:(c + 1) * 128], hT[k][:, :]) for k in range(8)],
                            [w_gate] + hT)
                sg = sig[c % 2]
                em.op("act", lambda e, sg=sg, bg=bg: e.activation(out=sg[:, :], in_=bg[:, :], func=AF.Sigmoid), [bg], [sg])
                em.op("dve", lambda e, c=c, sg=sg, bv=bv: e.tensor_tensor(out=aT[c][:, 30:30 + TT], in0=bv[:, :], in1=sg[:, :],
                                                                          op=ALU.mult), [bv, sg], [aT[c]])
                em.op("pool", lambda e, c=c: e.tensor_copy(out=ahalo[l][c][:, :], in_=aT[c][:, TT:TT + 30]),
                      [aT[c]], [ahalo[l][c]])
            release(2)
            for s in range(4):
                b = nb()
                em.mm_group(b, b[:, :], [(hT[k][:, s * 128:(s + 1) * 128], v8(w_pool)[:, k, :]) for k in range(8)],
                            [w_pool] + hT)
                em.op("act", lambda e, s=s, b=b: e.activation(out=upool[s][:, :], in_=b[:, :], func=AF.Copy), [b], [upool[s]])
            release(1)
            for j in range(31):
                build_diag(0, j)
            for c in range(4):
                b = nb()
                for j0 in range(0, 31, 4):
                    js = list(range(j0, min(j0 + 4, 31)))
                    dgs = [diag[(dbase + c * 31 + j) % ND] for j in js]
                    em.mm_group(b, b[:, :], [(dg[:, :], aT[c][:, j:j + TT]) for dg, j in zip(dgs, js)], dgs + [aT[c]],
                                first=(j0 == 0), last=(js[-1] == 30))
                    if c < 3:
                        for j in js:
                            build_diag(c + 1, j)
                em.op("act", lambda e, c=c, b=b: e.activation(out=cacc[c][:, :], in_=b[:, :], func=AF.Identity,
                                                             bias=smT[l][:, c:c + 1]), [b, smT[l]], [cacc[c]])
            dctr[0] += 124
            bm, bq = nb(), nb()
            for c in range(4):
                yb_, yq_ = ybf[c % 2], ysq[c % 2]
                em.op("act", lambda e, c=c, yb_=yb_: e.activation(out=yb_[:, :], in_=cacc[c][:, :], func=AF.Copy), [cacc[c]], [yb_])
                em.op("act", lambda e, c=c, yq_=yq_: e.activation(out=yq_[:, :], in_=cacc[c][:, :], func=AF.Square), [cacc[c]], [yq_])
                need = em._deps("pe", [ones_b, yb_, yq_], [bm, bq])
                em._wait("pe", need)
                i1 = nc.tensor.matmul(bm[:, :], ones_b[:, :], yb_[:, :], start=(c == 0), stop=(c == 3))
                i2 = nc.tensor.matmul(bq[:, :], ones_b[:, :], yq_[:, :], start=(c == 0), stop=(c == 3))
                em._inc("pe", i2)
                em.ninst += 2
                dep = ("pe", em.sem["pe"], em.cnt["pe"], "pe")
                em._mark(dep, [ones_b, yb_, yq_], [bm, bq])
            em.op("act", lambda e: e.activation(out=lnm[:, :], in_=bm[:, :], func=AF.Copy), [bm], [lnm])
            em.op("dve", lambda e: e.tensor_tensor(out=lnt[:, :], in0=lnm[:, :], in1=lnm[:, :], op=ALU.mult), [lnm], [lnt])
            em.op("dve", lambda e: e.tensor_tensor(out=lnt[:, :], in0=bq[:, :], in1=lnt[:, :], op=ALU.subtract), [bq, lnt], [lnt])
            for g in range(4):
                b = nb()
                for s in range(4):
                    cur = upool[s][:, g * 128:(g + 1) * 128]
                    o = b[:, s * 128:(s + 1) * 128]
                    if s == 0 and first:
                        em.mm_group(b, o, [(cur, tm[:, 8 + g, :])], [upool[s], tm])
                    else:
                        pb = phalo[l] if s == 0 else upool[s - 1]
                        em.mm_group(b, o, [(cur, tm[:, g, :]), (pb[:, g * 128:(g + 1) * 128], tm[:, 4 + g, :])],
                                    [upool[s], pb, tm])
                dd = dT[g % 2]
                em.op("dve", lambda e, dd=dd, b=b: e.tensor_copy(out=dd[:, :], in_=b[:, :]), [b], [dd])
                b2 = nb()
                if g == 0:
                    em.dma("sp", poolw[:, :, :], s_pool[l].rearrange("g c d -> c g d"), [cvb[l]["pool"]], [poolw], poolw)
                em.mm_group(b2, b2[:, :], [(poolw[:, g, :], dd[:, :])], [poolw, dd])
                em.op("dve", lambda e, g=g, b2=b2: e.tensor_scalar(out=mT[4 + g][:, :], in0=b2[:, :],
                                                                   scalar1=smT[l][:, 12 + g:13 + g], scalar2=None, op0=ALU.mult),
                      [b2, smT[l]], [mT[4 + g]])
            em.op("pool", lambda e: e.tensor_copy(out=phalo[l][:, :], in_=upool[3][:, :]), [upool[3]], [phalo[l]])
            wo = [take(), take()]
            load_gg(l, seq)
            obs = []
            for s in range(4):
                ob = [nb(), nb()]
                for h in range(2):
                    em.mm_group(ob[h], ob[h][:, :],
                                [(mT[k][:, s * 128:(s + 1) * 128], v8(wo[h])[:, k, :]) for k in range(4, 8)], [wo[h]] + mT[4:8],
                                first=True, last=False)
                obs.append(ob)
            wo_banks = (wo, obs)
            em.op("act", lambda e: e.activation(out=lnr[:, :], in_=lnt[:, :], func=AF.Sqrt, bias=EPS), [lnt], [lnr])
            em.op("dve", lambda e: e.reciprocal(out=lnr[:, :], in_=lnr[:, :]), [lnr], [lnr])
            for c in range(4):
                z = zt[c % 2]
                em.op("dve", lambda e, c=c, z=z: e.tensor_tensor(out=z[:, :], in0=cacc[c][:, :], in1=lnm[:, :], op=ALU.subtract),
                      [cacc[c], lnm], [z])
                em.op("dve", lambda e, z=z: e.tensor_tensor(out=z[:, :], in0=z[:, :], in1=lnr[:, :], op=ALU.mult), [z, lnr], [z])
                em.op("act", lambda e, c=c, z=z: e.activation(out=mT[c][:, :], in_=z[:, :], func=AF.Silu,
                                                             scale=smT[l][:, 4 + c:5 + c], bias=smT[l][:, 8 + c:9 + c]),
                      [z, smT[l]], [mT[c]])
            return wo_banks

        def mixer_tail(i, l, wo, obs):
            g = ggb[0]
            for s in range(4):
                ob = obs[s]
                for h in range(2):
                    em.mm_group(ob[h], ob[h][:, :],
                                [(mT[k][:, s * 128:(s + 1) * 128], v8(wo[h])[:, k, :]) for k in range(4)], [wo[h]] + mT[0:4],
                                first=False, last=True)
            release(2)
            for s in range(4):
                for h in range(2):
                    em.op("act", lambda e, s=s, h=h: e.activation(out=xn[s][:, h * 512:(h + 1) * 512], in_=obs[s][h][:, :],
                                                                 func=AF.Square, accum_out=st4[:, 2 * s + h:2 * s + h + 1]),
                          [obs[s][h]], [xn[s], st4])
            em.op("dve", lambda e: e.tensor_tensor(out=st4[:, 8:12], in0=st4[:, 0:8:2], in1=st4[:, 1:8:2], op=ALU.add), [st4], [st4])
            em.op("act", lambda e: e.activation(out=st4[:, 12:16], in_=st4[:, 8:12], func=AF.Sqrt, scale=1.0 / D, bias=EPS),
                  [st4], [st4])
            em.op("dve", lambda e: e.reciprocal(out=st4[:, 12:16], in_=st4[:, 12:16]), [st4], [st4])
            for s in range(4):
                pt = ptmps[s % 2]
                for h in range(2):
                    em.op("dve", lambda e, s=s, h=h, pt=pt: e.scalar_tensor_tensor(
                        out=pt[:, h * 512:(h + 1) * 512], in0=obs[s][h][:, :], scalar=st4[:, 12 + s:13 + s],
                        in1=g[:, h * 512:(h + 1) * 512], op0=ALU.mult, op1=ALU.mult), [obs[s][h], st4, g], [pt])
                em.op("pool" if s < 3 else "dve",
                      lambda e, s=s, pt=pt: e.tensor_tensor(out=xs[s][:, :], in0=xs[s][:, :], in1=pt[:, :], op=ALU.add),
                      [xs[s], pt], [xs[s]])
            for s in range(4):
                pre_elem(s)

        def ffn(i, l):
            seq = i // tiles_per_seq
            first = (i % tiles_per_seq) == 0
            if i == 0 and l + 1 < L:
                convert_weights(l + 1)
            pre_finish(l, seq, 4, 3)
            if first:
                em.op("pool", lambda e: e.memset(fhalo[l][:, :, :], 0.0), [], [fhalo[l]])
            fh = fhalo[l]
            W0, W1 = fwT[l][:, 0:44], fwT[l][:, 44:88]
            em.op("pool", lambda e: e.tensor_tensor(out=fcorr[:, :, 0], in0=fh[:, :, 0], in1=W0, op=ALU.mult), [fh, fwT[l]], [fcorr])
            em.op("pool", lambda e: e.tensor_tensor(out=fcorr[:, :, 1], in0=fh[:, :, 1], in1=W0, op=ALU.mult), [fh, fwT[l]], [fcorr])
            em.op("pool", lambda e: e.tensor_tensor(out=fcorr2[:, :], in0=fh[:, :, 1], in1=W1, op=ALU.mult), [fh, fwT[l]], [fcorr2])
            em.op("pool", lambda e: e.tensor_tensor(out=fcorr[:, :, 0], in0=fcorr[:, :, 0], in1=fcorr2[:, :], op=ALU.add),
                  [fcorr, fcorr2], [fcorr])
            pieces = {}

            def stage_ab(j):
                q, jj = j // 4, j % 4
                n = 512 if q < 5 else 256
                if jj == 0:
                    pieces[q] = (take(), take())
                banks = []
                for gi, w in enumerate(pieces[q]):
                    b = nb()
                    if j == 0 and gi == 0:
                        for k in range(8):
                            em.mm_group(b, b[:, :], [(v8(w, n)[:, k, 0:128], hT[k][:, :])], [w, hT[k]],
                                        first=(k == 0), last=(k == 7))
                    else:
                        em.mm_group(b, b[:, :], [(v8(w, n)[:, k, jj * 128:(jj + 1) * 128], hT[k][:, :]) for k in range(8)],
                                    [w] + hT)
                    banks.append(b)
                if jj == n // 128 - 1:
                    release(2)
                for gi, b in enumerate(banks):
                    ch = gi * NF + j
                    acc = facc[(j % 4) * 2 + gi]
                    w2 = fwT[l][:, 2 * 44 + ch:2 * 44 + ch + 1]
                    em.op("act", lambda e, acc=acc, b=b, w2=w2, ch=ch: e.activation(out=acc[:, :], in_=b[:, :], func=AF.Identity,
                                                                                   scale=w2, bias=fbT[l][:, ch:ch + 1]),
                          [b, fwT[l], fbT[l]], [acc])
                    em.op("act", lambda e, ch=ch, b=b: e.activation(out=fh[:, ch, 0:2], in_=b[:, TT - 2:TT], func=AF.Copy), [b], [fh])
                for gi, b in enumerate(banks):
                    ch = gi * NF + j
                    acc = facc[(j % 4) * 2 + gi]
                    w0 = fwT[l][:, 0 * 44 + ch:0 * 44 + ch + 1]
                    w1 = fwT[l][:, 1 * 44 + ch:1 * 44 + ch + 1]
                    em.op("dve", lambda e, acc=acc, b=b, w1=w1: e.scalar_tensor_tensor(out=acc[:, 1:TT], in0=b[:, 0:TT - 1], scalar=w1,
                                                                                      in1=acc[:, 1:TT], op0=ALU.mult, op1=ALU.add),
                          [b, fwT[l], acc], [acc])
                    em.op("dve", lambda e, acc=acc, b=b, w0=w0: e.scalar_tensor_tensor(out=acc[:, 2:TT], in0=b[:, 0:TT - 2], scalar=w0,
                                                                                      in1=acc[:, 2:TT], op0=ALU.mult, op1=ALU.add),
                          [b, fwT[l], acc], [acc])
                for gi in range(2):
                    ch = gi * NF + j
                    acc = facc[(j % 4) * 2 + gi]
                    em.op("pool", lambda e, acc=acc, ch=ch: e.tensor_tensor(out=acc[:, 0:2], in0=acc[:, 0:2], in1=fcorr[:, ch, :],
                                                                            op=ALU.add), [acc, fcorr], [acc])

            def stage_d(j):
                av, ag = facc[(j % 4) * 2], facc[(j % 4) * 2 + 1]
                em.op("act", lambda e: e.activation(out=ag[:, :], in_=ag[:, :], func=AF.Silu), [ag], [ag])
                em.op("pool", lambda e: e.tensor_tensor(out=hid[j][:, :], in0=av[:, :], in1=ag[:, :], op=ALU.mult),
                      [av, ag], [hid[j]])

            for j in range(NF + 2):
                if j < NF:
                    stage_ab(j)
                if j >= 2:
                    stage_d(j - 2)
            wd = [take() for _ in range(6)]
            for s in range(4):
                ob = [nb(), nb()]
                for h in range(2):
                    pairs = []
                    for j in range(NF):
                        q, jj = j // 4, j % 4
                        nj = 4 if q < 5 else 2
                        pairs.append((hid[j][:, s * 128:(s + 1) * 128], v_down(wd[q], nj)[:, jj, h * 512:(h + 1) * 512]))
                    if s == 0 and h == 0:
                        em.mm_group(ob[h], ob[h][:, :], pairs[:16], wd + hid[:16], first=True, last=False)
                        em.mm_group(ob[h], ob[h][:, :], pairs[16:], wd + hid[16:], first=False, last=True)
                    else:
                        em.mm_group(ob[h], ob[h][:, :], pairs, wd + hid)
                if l < L - 1:
                    postnorm(5, s, ob, pool_add=(s < 3))
                    pre_elem(s)
                else:
                    r0 = i * TT + s * 128
                    postnorm(5, s, ob, final=(r0, r0 + TT if i + 1 < n_tiles else None))
                    if i + 1 < n_tiles:
                        pre_elem(s)
            release(6)

        for s in range(4):
            em.dma("sp", xs[s][:, :], x_d[s * 128:(s + 1) * 128, :], [], [xs[s]], xs[s])
            pre_elem(s)
        for i in range(n_tiles):
            for l in range(L):
                wo, obs = mixer(i, l)
                mixer_tail(i, l, wo, obs)
                ffn(i, l)
        for s in range(4):
            nc.sync.wait_ge(yb[s].dsem, yb[s].dcnt)
        if needed is not None:
            print("instructions:", em.ninst, "waits:", em.nwaits, "milestones:", em.cnt, "incs:", em.real)
    return nc, em.waited


def _consts():
    ident = np.eye(128, dtype=np.float32)
    tm = np.zeros((12, 128, 128), np.float32)
    tp = np.arange(128)[:, None]
    t = np.arange(128)[None, :]
    for g, w in enumerate(POOL_W):
        d = t - tp
        tm[g] = ((d >= 0) & (d < w)) / w - (d == 0)
        d2 = t + 128 - tp
        tm[4 + g] = ((d2 > 0) & (d2 < w)) / w
        cnt = np.minimum(t + 1, w).astype(np.float32)
        tm[8 + g] = ((d >= 0) & (d < w)) / cnt - (d == 0)
    return ident, tm


_NAMES = ["ada_w", "ada_b", "pre_mix_g", "post_mix_g", "w_in", "conv_w", "conv_b", "conv_ln_g", "conv_ln_b",
          "pool_w", "pool_scale", "w_out", "pre_ffn_g", "post_ffn_g", "ffn_up", "ffn_conv_w", "ffn_conv_b", "ffn_down"]


def run(inputs, n_cores, L, seq_per_core, seq_len, dbg=None):
    nc = build(L, seq_per_core, seq_len, dbg=dbg)
    ident, tm = _consts()
    x = np.ascontiguousarray(inputs["x"], dtype=np.float32)
    c = np.ascontiguousarray(inputs["c"], dtype=np.float32)
    in_maps = []
    for r in range(n_cores):
        m = {"x": x[r * seq_per_core:(r + 1) * seq_per_core].reshape(seq_per_core * seq_len, D),
             "c": c[r * seq_per_core:(r + 1) * seq_per_core], "ident": ident, "tmats": tm}
        for n in _NAMES:
            m[n] = np.ascontiguousarray(inputs[n][:L], dtype=np.float32)
        in_maps.append(m)
    res = run_bass_kernel_spmd(nc, in_maps, core_ids=list(range(n_cores)))
    y = np.concatenate([r["y"].reshape(seq_per_core, seq_len, D) for r in res.results], axis=0)
    if dbg is not None:
        return y, [r["dbg"] for r in res.results]
    return y


def kernel(**inputs):
    return run(inputs, 8, 4, 2, 2048).astype(np.float32)
```

```python
import contextlib
import numpy as np
import concourse.bass as bass
import concourse.mybir as mybir
from concourse.bass_utils import run_bass_kernel_spmd

F32 = mybir.dt.float32
BF16 = mybir.dt.bfloat16
AF = mybir.ActivationFunctionType
ALU = mybir.AluOpType

D = 1024
DC = 512
DFF = 2816
NF = 22
TT = 512
EPS = 1e-6
NSLOT = 10
POOL_W = (2, 4, 8, 16)


class Buf:
    def __init__(self, name, t, psum=False):
        self.name = name
        self.t = t
        self.psum = psum
        self.writer = None
        self.readers = []
        self.dsem = None
        self.dcnt = 0

    def __getitem__(self, k):
        return self.t[k]


class Em:
    def __init__(self, nc, stack, needed=None):
        self.nc = nc
        self.stack = stack
        self.needed = needed
        self.waited = set()
        self.real = {}
        self.realmap = {}
        self.eng = {"pe": nc.tensor, "act": nc.scalar, "dve": nc.vector, "pool": nc.gpsimd, "sp": nc.sync}
        self.sem = {k: stack.enter_context(nc.semaphore("sem_" + k)) for k in self.eng}
        self.cnt = {k: 0 for k in self.eng}
        self.real = {k: 0 for k in self.eng}
        self.seen = {k: {} for k in self.eng}
        self.nwaits = 0
        self.ninst = 0

    def sb(self, name, shape, dt):
        return Buf(name, self.nc.alloc_sbuf_tensor(name, list(shape), dt))

    def ps(self, name, shape, dt=F32):
        return Buf(name, self.nc.alloc_psum_tensor(name, list(shape), dt), psum=True)

    def _deps(self, engine, reads, writes):
        deps = []
        for b in reads:
            if b.writer is not None:
                deps.append((b.writer, True))
            if b.psum:
                deps += [(r, True) for r in b.readers]
        for b in writes:
            if b.writer is not None:
                deps.append((b.writer, b.psum))
            deps += [(r, b.psum) for r in b.readers]
        need = {}
        for (key, sem, val, src), raw in deps:
            if src == engine:
                if engine == "pe" or engine == "sp":
                    continue
                if not raw or val < self.cnt[engine]:
                    continue
            if key not in need or need[key][1] < val:
                need[key] = (sem, val)
        return need

    def _wait(self, engine, need):
        e = self.eng[engine]
        seen = self.seen[engine]
        for key, (sem, val) in need.items():
            if seen.get(key, 0) >= val:
                continue
            rv = val
            if key in self.eng:
                self.waited.add((key, val))
                if self.needed is not None:
                    rv = self.realmap[(key, val)]
            e.wait_ge(sem, rv)
            seen[key] = val
            self.nwaits += 1

    def _mark(self, dep, reads, writes):
        for b in reads:
            if b.psum:
                b.writer = dep
                b.readers = []
            else:
                b.readers = [r for r in b.readers if r[0] != dep[0]] + [dep]
        for b in writes:
            b.writer = dep
            b.readers = []

    def _inc(self, engine, inst):
        self.cnt[engine] += 1
        if self.needed is None or (engine, self.cnt[engine]) in self.needed:
            inst.then_inc(self.sem[engine], 1)
            self.real[engine] += 1
            self.realmap[(engine, self.cnt[engine])] = self.real[engine]

    def op(self, engine, fn, reads=(), writes=()):
        need = self._deps(engine, reads, writes)
        self._wait(engine, need)
        inst = fn(self.eng[engine])
        self._inc(engine, inst)
        self.ninst += 1
        dep = (engine, self.sem[engine], self.cnt[engine], engine)
        self._mark(dep, reads, writes)

    def multi(self, engine, fns, reads=(), writes=()):
        need = self._deps(engine, reads, writes)
        self._wait(engine, need)
        inst = None
        for fn in fns:
            inst = fn(self.eng[engine])
            self.ninst += 1
        self._inc(engine, inst)
        dep = (engine, self.sem[engine], self.cnt[engine], engine)
        self._mark(dep, reads, writes)

    def mm_group(self, out_buf, out_ap, pairs, reads, first=True, last=True):
        need = self._deps("pe", reads, [out_buf])
        self._wait("pe", need)
        n = len(pairs)
        inst = None
        for i, (l, r) in enumerate(pairs):
            inst = self.nc.tensor.matmul(out_ap, l, r, start=(first and i == 0), stop=(last and i == n - 1))
            self.ninst += 1
        self._inc("pe", inst)
        dep = ("pe", self.sem["pe"], self.cnt["pe"], "pe")
        self._mark(dep, reads, [out_buf])

    def dma(self, queue, out_ap, in_ap, reads, writes, owner):
        if owner.dsem is None:
            owner.dsem = self.stack.enter_context(self.nc.semaphore("dsem_" + owner.name))
        need = self._deps(queue, reads, writes)
        if owner.dcnt > 0:
            key = ("d", owner.name)
            if key not in need or need[key][1] < owner.dcnt:
                need[key] = (owner.dsem, owner.dcnt)
        self._wait(queue, need)
        inst = self.eng[queue].dma_start(out=out_ap, in_=in_ap)
        owner.dcnt += 16
        inst.then_inc(owner.dsem, 16)
        self.ninst += 1
        dep = (("d", owner.name), owner.dsem, owner.dcnt, None)
        self._mark(dep, reads, writes)
        return dep


def build(L, n_seq, seq_len, dbg=None):
    _, needed = _build(L, n_seq, seq_len, None)
    nc, _ = _build(L, n_seq, seq_len, needed)
    return nc


def _build(L, n_seq, seq_len, needed):
    P = 128
    dbg = None
    nc = bass.Bass("TRN2", target_bir_lowering=False)
    NTOK = n_seq * seq_len
    tiles_per_seq = seq_len // TT
    n_tiles = n_seq * tiles_per_seq

    def din(name, shape):
        return nc.dram_tensor(name, list(shape), F32, kind="ExternalInput").ap()

    x_d = din("x", [NTOK, D])
    c_d = din("c", [n_seq, D])
    ada_w = din("ada_w", [L, D, 6 * D])
    ada_b = din("ada_b", [L, 6 * D])
    pre_mix_g = din("pre_mix_g", [L, D])
    post_mix_g = din("post_mix_g", [L, D])
    w_in = din("w_in", [L, D, 3 * DC])
    conv_w = din("conv_w", [L, 31, DC])
    conv_b = din("conv_b", [L, DC])
    conv_ln_g = din("conv_ln_g", [L, DC])
    conv_ln_b = din("conv_ln_b", [L, DC])
    pool_w = din("pool_w", [L, 4, 128, 128])
    pool_scale = din("pool_scale", [L, DC])
    w_out = din("w_out", [L, D, D])
    pre_ffn_g = din("pre_ffn_g", [L, D])
    post_ffn_g = din("post_ffn_g", [L, D])
    ffn_up = din("ffn_up", [L, D, 2 * DFF])
    ffn_conv_w = din("ffn_conv_w", [L, 3, 2 * DFF])
    ffn_conv_b = din("ffn_conv_b", [L, 2 * DFF])
    ffn_down = din("ffn_down", [L, DFF, D])
    ident_d = din("ident", [P, P])
    tm_d = din("tmats", [12, P, P])
    y_d = nc.dram_tensor("y", [NTOK, D], F32, kind="ExternalOutput").ap()
    dbg_d = None
    if dbg is not None:
        dbg_d = nc.dram_tensor("dbg", [P, dbg], F32, kind="ExternalOutput").ap()

    def dscr(name, shape, dt=BF16):
        return nc.dram_tensor(name, list(shape), dt, kind="Internal").ap()

    s_in = dscr("s_in", [L, D, 3 * DC])
    s_out = dscr("s_out", [L, D, D])
    s_pool = dscr("s_pool", [L, 4, P, P])
    s_up = dscr("s_up", [L, D, 2 * DFF])
    s_down = dscr("s_down", [L, DFF, D])
    modscr = dscr("modscr", [L, 12, D], F32)

    with contextlib.ExitStack() as stack:
        em = Em(nc, stack, needed)

        xs = [em.sb(f"x{s}", [P, D], F32) for s in range(4)]
        hT = [em.sb(f"hT{k}", [P, TT], BF16) for k in range(8)]
        hid = [em.sb(f"hid{j}", [P, TT], BF16) for j in range(NF)]
        ring = [em.sb(f"ring{i}", [P, 4096], BF16) for i in range(NSLOT)]
        poolw = em.sb("poolw", [P, 4, P], BF16)
        xn = [em.sb(f"xn{i}", [P, D], BF16) for i in range(4)]
        import os
        JK = os.environ.get("K_JUNK", "0")
        junk = em.sb("junk", [P, D], BF16) if JK != "0" else None
        aT = [em.sb(f"aT{c}", [P, 30 + TT], BF16) for c in range(4)]
        ND = 32
        diag = [em.sb(f"diag{i}", [P, P], BF16) for i in range(ND)]
        cacc = [em.sb(f"cacc{c}", [P, TT], F32) for c in range(4)]
        upool = [em.sb(f"upool{s}", [P, DC], BF16) for s in range(4)]
        dT, ybf, ysq = hid[8:10], hid[10:12], hid[12:14]
        lnm = em.sb("lnm", [P, TT], F32)
        lnr = em.sb("lnr", [P, TT], F32)
        lnt = em.sb("lnt", [P, TT], F32)
        zt = [em.sb(f"zt{i}", [P, TT], F32) for i in range(2)]
        sig = zt
        mT = hid[0:8]
        ggb = [em.sb(f"ggb{g}", [P, D], F32) for g in range(2)]
        facc = cacc + zt + [lnt, lnr]
        ptmps = [em.sb(f"ptmp{i}", [P, D], F32) for i in range(2)]
        ptmp = ptmps[0]
        st4 = em.sb("st4", [P, 16], F32)
        stat = [em.sb(f"stat{i}", [P, 2], F32) for i in range(4)]
        pst = em.sb("pst", [P, 8], F32)
        fcorr = em.sb("fcorr", [P, 2 * NF, 2], F32)
        fcorr2 = em.sb("fcorr2", [P, 2 * NF], F32)
        ident_f = em.sb("ident_f", [P, P], F32)
        ident_b = em.sb("ident_b", [P, P], BF16)
        ones_b = em.sb("ones_b", [P, P], BF16)
        tm = em.sb("tm", [P, 12, P], BF16)
        stage = em.sb("stage", [P, P], F32)
        cT = em.sb("cT", [P, 16], BF16)
        colT = [em.sb(f"colT{l}", [P, 96], F32) for l in range(L)]
        cwT = [em.sb(f"cwT{l}", [P, 124], F32) for l in range(L)]
        smT = [em.sb(f"smT{l}", [P, 16], F32) for l in range(L)]
        fwT = [em.sb(f"fwT{l}", [P, 132], F32) for l in range(L)]
        fbT = [em.sb(f"fbT{l}", [P, 44], F32) for l in range(L)]
        ahalo = [[em.sb(f"ahalo{l}_{c}", [P, 30], BF16) for c in range(4)] for l in range(L)]
        phalo = [em.sb(f"phalo{l}", [P, DC], BF16) for l in range(L)]
        fhalo = [em.sb(f"fhalo{l}", [P, 2 * NF, 2], F32) for l in range(L)]
        trow, brow, grow = ptmp, ggb[0], ggb[1]
        bank = [em.ps(f"bank{i}", [P, 512], F32) for i in range(8)]
        if needed is not None:
            print("sbuf bytes remaining:", nc.sbuf_bytes_remaining)

        bank_rr = [0]
        dctr = [0]

        def nb():
            b = bank[bank_rr[0] % 8]
            bank_rr[0] += 1
            return b

        cvb = [{w: Buf(f"cv{l}_{w}", None) for w in ("in", "pool", "out", "up", "down")} for l in range(L)]
        modb = [Buf(f"modb{l}", None) for l in range(L)]
        yb = [Buf(f"yb{s}", None) for s in range(4)]

        def convert_weights(l):
            for w, dst, src in (("in", s_in[l], w_in[l]), ("pool", s_pool[l], pool_w[l]), ("out", s_out[l], w_out[l]),
                                ("up", s_up[l], ffn_up[l]), ("down", s_down[l], ffn_down[l])):
                em.dma("pool", dst, src, [], [cvb[l][w]], cvb[l][w])

        em.dma("sp", ident_f[:], ident_d, [], [ident_f], ident_f)
        em.op("dve", lambda e: e.tensor_copy(out=ident_b[:], in_=ident_f[:]), [ident_f], [ident_b])
        em.op("dve", lambda e: e.memset(ones_b[:], 1.0 / 512.0), [], [ones_b])
        for i in range(12):
            em.dma("sp", stage[:], tm_d[i], [], [stage], stage)
            em.op("dve", lambda e, i=i: e.tensor_copy(out=tm[:, i, :], in_=stage[:]), [stage], [tm])

        def transpose_rows(src_ap, nrows, dst_buf, dst_ap, evac_engine="dve", func=None):
            em.dma("sp", stage[0:nrows, :], src_ap, [], [stage], stage)
            b = nb()
            em.op("pe", lambda e: e.transpose(b[:, 0:nrows], stage[0:nrows, :], ident_f[0:nrows, 0:nrows]),
                  [stage, ident_f], [b])
            if func is None:
                em.op("dve", lambda e: e.tensor_copy(out=dst_ap, in_=b[:, 0:nrows]), [b], [dst_buf])
            else:
                em.op("act", lambda e: e.activation(out=dst_ap, in_=b[:, 0:nrows], func=func), [b], [dst_buf])

        transpose_rows(c_d.rearrange("b (k p) -> (b k) p", p=P), n_seq * 8, cT, cT[:, 0:n_seq * 8], func=AF.Silu)

        for l in range(L):
            transpose_rows(conv_w[l].rearrange("k (c p) -> (k c) p", p=P), 124, cwT[l], cwT[l][:, :])
            for i, v in enumerate((conv_b, conv_ln_g, conv_ln_b, pool_scale)):
                transpose_rows(v[l].rearrange("(c p) -> c p", p=P), 4, smT[l], smT[l][:, 4 * i:4 * i + 4])
            fw = ffn_conv_w[l].rearrange("k (j p) -> (k j) p", p=P)
            transpose_rows(fw[0:128], 128, fwT[l], fwT[l][:, 0:128])
            transpose_rows(fw[128:132], 4, fwT[l], fwT[l][:, 128:132])
            transpose_rows(ffn_conv_b[l].rearrange("(j p) -> j p", p=P), 44, fbT[l], fbT[l][:, :])

        slot_i = [0]

        def next_slot():
            s = ring[slot_i[0] % NSLOT]
            slot_i[0] += 1
            return s

        gains = {1: pre_mix_g, 2: post_mix_g, 4: pre_ffn_g, 5: post_ffn_g}
        for l in range(L):
            for v in range(6):
                halves = []
                for h in range(2):
                    sl = next_slot()
                    c0 = v * D + h * 512
                    em.dma("pool", sl[:, :].rearrange("p (k n) -> p k n", k=8),
                           ada_w[l][:, c0:c0 + 512].rearrange("(k p) n -> p k n", p=P), [], [sl], sl)
                    b = nb()
                    slv = sl[:, :].rearrange("p (k n) -> p k n", k=8)
                    cTv = cT[:, 0:n_seq * 8].rearrange("p (b k) -> p b k", k=8)
                    em.mm_group(b, b[0:n_seq, :], [(cTv[:, :, k], slv[:, k, :]) for k in range(8)], [cT, sl])
                    halves.append(b)
                em.dma("sp", brow[0:n_seq, :], ada_b[l:l + 1, v * D:(v + 1) * D].broadcast_to([n_seq, D]), [], [brow], brow)
                for h in range(2):
                    em.op("dve", lambda e, h=h: e.tensor_tensor(out=trow[0:n_seq, h * 512:(h + 1) * 512],
                                                                in0=halves[h][0:n_seq, :],
                                                                in1=brow[0:n_seq, h * 512:(h + 1) * 512], op=ALU.add),
                          [halves[h], brow], [trow])
                if v in gains:
                    em.dma("sp", grow[0:n_seq, :], gains[v][l:l + 1, :].broadcast_to([n_seq, D]), [], [grow], grow)
                    em.op("dve", lambda e: e.scalar_tensor_tensor(out=trow[0:n_seq, :], in0=trow[0:n_seq, :], scalar=1.0,
                                                                  in1=grow[0:n_seq, :], op0=ALU.add, op1=ALU.mult),
                          [trow, grow], [trow])
                em.dma("sp", modscr[l, 2 * v:2 * v + n_seq, :], trow[0:n_seq, :], [trow], [modb[l]], modb[l])
            em.dma("sp", stage[0:96, :], modscr[l].rearrange("q (c p) -> (q c) p", p=P), [modb[l]], [stage], stage)
            b = nb()
            em.op("pe", lambda e: e.transpose(b[:, 0:96], stage[0:96, :], ident_f[0:96, 0:96]), [stage, ident_f], [b])
            em.op("dve", lambda e, l=l: e.tensor_copy(out=colT[l][:, :], in_=b[:, 0:96]), [b], [colT[l]])
            if l == 0:
                convert_weights(0)


        def v8(sl, n=512):
            return sl[:, 0:8 * n].rearrange("p (k n) -> p k n", k=8)

        def v_down(sl, nj):
            return sl[:, 0:nj * D].rearrange("p (j n) -> p j n", j=nj)

        sched = []
        for i in range(n_tiles):
            for l in range(L):
                for q in range(3):
                    sched.append((cvb[l]["in"], [(lambda sl: v8(sl),
                                       s_in[l][:, q * 512:(q + 1) * 512].rearrange("(k p) n -> p k n", p=P))]))
                for q in range(2):
                    sched.append((cvb[l]["out"], [(lambda sl: v8(sl),
                                       s_out[l][:, q * 512:(q + 1) * 512].rearrange("(k p) n -> p k n", p=P))]))
                for q in range(6):
                    n = 512 if q < 5 else 256
                    for g in range(2):
                        c0 = g * DFF + q * 512
                        sched.append((cvb[l]["up"], [(lambda sl, n=n: v8(sl, n),
                                           s_up[l][:, c0:c0 + n].rearrange("(k p) n -> p k n", p=P))]))
                for q in range(6):
                    nj = 4 if q < 5 else 2
                    sched.append((cvb[l]["down"], [(lambda sl, nj=nj: v_down(sl, nj),
                                       s_down[l][q * 512:q * 512 + nj * P, :].rearrange("(j p) n -> p j n", p=P))]))
        issued = [0]
        consumed = [0]
        released = [0]
        base_slot = slot_i[0]

        def pump():
            while issued[0] < len(sched) and issued[0] < released[0] + NSLOT:
                q = issued[0]
                cv, dmas = sched[q]
                sl = ring[(base_slot + q) % NSLOT]
                for dst_fn, src in dmas:
                    em.dma("sp", dst_fn(sl), src, [cv], [sl], sl)
                issued[0] += 1

        def take():
            p = consumed[0]
            pump()
            assert issued[0] > p, "ring too small for the pieces held at once"
            consumed[0] += 1
            return ring[(base_slot + p) % NSLOT]

        def release(n):
            released[0] += n
            assert released[0] <= consumed[0]
            pump()

        def pre_elem(s):
            xb = xn[s]
            jb = junk if JK in ("1", "3") else xb
            em.op("act", lambda e: e.activation(out=jb[:, :], in_=xs[s][:, :], func=AF.Square, accum_out=pst[:, s:s + 1]),
                  [xs[s]], [jb, pst])
            em.op("act", lambda e: e.activation(out=pst[:, 4 + s:5 + s], in_=pst[:, s:s + 1], func=AF.Sqrt, scale=1.0 / D, bias=EPS),
                  [pst], [pst])
            em.op("dve", lambda e: e.reciprocal(out=pst[:, 4 + s:5 + s], in_=pst[:, 4 + s:5 + s]), [pst], [pst])
            em.op("dve", lambda e: e.tensor_scalar(out=xb[:, :], in0=xs[s][:, :], scalar1=pst[:, 4 + s:5 + s], scalar2=None,
                                                   op0=ALU.mult), [xs[s], pst], [xb])

        def pre_finish(l, seq, vsc, vsh):
            ca = (vsc * 2 + seq) * 8
            cb = (vsh * 2 + seq) * 8
            tb = [nb(), nb(), nb(), nb()]
            for s in range(4):
                xb = xn[s]
                fns = []
                for k in range(8):
                    bv = tb[k // 2][:, :].bitcast(BF16)
                    o = bv[:, (k % 2) * 512 + s * 128:(k % 2) * 512 + (s + 1) * 128]
                    fns.append(lambda e, o=o, xb=xb, k=k: e.transpose(o, xb[:, k * 128:(k + 1) * 128], ident_b[:, :]))
                em.multi("pe", fns, [xb, ident_b], tb)
            for k in range(8):
                b = tb[k // 2]
                bv = b[:, :].bitcast(BF16)
                src = bv[:, (k % 2) * 512:(k % 2) * 512 + 512]
                if k % 2 == 0:
                    em.op("act", lambda e, k=k, src=src: e.activation(out=hT[k][:, :], in_=src, func=AF.Identity,
                                                                     scale=colT[l][:, ca + k:ca + k + 1],
                                                                     bias=colT[l][:, cb + k:cb + k + 1]), [b, colT[l]], [hT[k]])
                else:
                    em.op("dve", lambda e, k=k, src=src: e.tensor_scalar(out=hT[k][:, :], in0=src, scalar1=colT[l][:, ca + k:ca + k + 1],
                                                                        scalar2=colT[l][:, cb + k:cb + k + 1], op0=ALU.mult,
                                                                        op1=ALU.add), [b, colT[l]], [hT[k]])

        def postnorm(vg, s, ob, final=None, pool_add=False):
            st = stat[s]
            g = ggb[0 if vg == 2 else 1]
            for h in range(2):
                jb2 = junk if JK in ("2", "3") else ptmp
                em.op("act", lambda e, h=h: e.activation(out=jb2[:, h * 512:(h + 1) * 512], in_=ob[h][:, :], func=AF.Square,
                                                         accum_out=st[:, h:h + 1]), [ob[h]], [jb2, st])
            em.op("dve", lambda e: e.tensor_tensor(out=st[:, 0:1], in0=st[:, 0:1], in1=st[:, 1:2], op=ALU.add), [st], [st])
            em.op("act", lambda e: e.activation(out=st[:, 1:2], in_=st[:, 0:1], func=AF.Sqrt, scale=1.0 / D, bias=EPS), [st], [st])
            em.op("dve", lambda e: e.reciprocal(out=st[:, 1:2], in_=st[:, 1:2]), [st], [st])
            for h in range(2):
                em.op("dve", lambda e, h=h: e.scalar_tensor_tensor(out=ptmp[:, h * 512:(h + 1) * 512], in0=ob[h][:, :],
                                                                   scalar=st[:, 1:2], in1=g[:, h * 512:(h + 1) * 512],
                                                                   op0=ALU.mult, op1=ALU.mult), [ob[h], st, g], [ptmp])
            if final is None:
                em.op("pool" if pool_add else "dve",
                      lambda e: e.tensor_tensor(out=xs[s][:, :], in0=xs[s][:, :], in1=ptmp[:, :], op=ALU.add),
                      [xs[s], ptmp], [xs[s]])
            else:
                r0, nr0 = final
                em.op("dve", lambda e: e.tensor_tensor(out=ptmp[:, :], in0=xs[s][:, :], in1=ptmp[:, :], op=ALU.add),
                      [xs[s], ptmp], [ptmp])
                em.dma("sp", y_d[r0:r0 + 128, :], ptmp[:, :], [ptmp], [yb[s]], yb[s])
                if nr0 is not None:
                    em.dma("sp", xs[s][:, :], x_d[nr0:nr0 + 128, :], [], [xs[s]], xs[s])

        def load_gg(l, seq):
            for gi, v in enumerate((2, 5)):
                em.dma("sp", ggb[gi][:, :], modscr[l, 2 * v + seq:2 * v + seq + 1, :].broadcast_to([P, D]),
                       [modb[l]], [ggb[gi]], ggb[gi])

        def mixer(i, l):
            seq = i // tiles_per_seq
            first = (i % tiles_per_seq) == 0
            dbase = dctr[0]

            def build_diag(c, j):
                dg = diag[(dbase + c * 31 + j) % ND]
                w = cwT[l][:, j * 4 + c:j * 4 + c + 1]
                if j % 2 == 0:
                    em.op("dve", lambda e: e.tensor_scalar(out=dg[:, :], in0=ident_b[:, :], scalar1=w, scalar2=None, op0=ALU.mult),
                          [ident_b, cwT[l]], [dg])
                else:
                    em.op("act", lambda e: e.activation(out=dg[:, :], in_=ident_b[:, :], func=AF.Copy, scale=w),
                          [ident_b, cwT[l]], [dg])

            pre_finish(l, seq, 1, 0)
            w_val, w_gate, w_pool = take(), take(), take()
            for c in range(4):
                if first:
                    em.op("pool", lambda e, c=c: e.memset(aT[c][:, 0:30], 0.0), [], [aT[c]])
                else:
                    em.op("pool", lambda e, c=c: e.tensor_copy(out=aT[c][:, 0:30], in_=ahalo[l][c][:, :]),
                          [ahalo[l][c]], [aT[c]])
                bv, bg = nb(), nb()
                if c == 0:
                    for k in range(8):
                        em.mm_group(bv, bv[:, :], [(v8(w_val)[:, k, 0:128], hT[k][:, :])], [w_val, hT[k]],
                                    first=(k == 0), last=(k == 7))
                else:
                    em.mm_group(bv, bv[:, :], [(v8(w_val)[:, k, c * 128:(c + 1) * 128], hT[k][:, :]) for k in range(8)],
                                [w_val] + hT)
                em.mm_group(bg, bg[:, :], [(v8(w_gate)[:, k, c * 128:(c + 1) * 128], hT[k][:, :]) for k in range(8)],
                            [w_gate] + hT)
                sg = sig[c % 2]
                em.op("act", lambda e, sg=sg, bg=bg: e.activation(out=sg[:, :], in_=bg[:, :], func=AF.Sigmoid), [bg], [sg])
                em.op("dve", lambda e, c=c, sg=sg, bv=bv: e.tensor_tensor(out=aT[c][:, 30:30 + TT], in0=bv[:, :], in1=sg[:, :],
                                                                          op=ALU.mult), [bv, sg], [aT[c]])
                em.op("pool", lambda e, c=c: e.tensor_copy(out=ahalo[l][c][:, :], in_=aT[c][:, TT:TT + 30]),
                      [aT[c]], [ahalo[l][c]])
            release(2)
            for s in range(4):
                b = nb()
                em.mm_group(b, b[:, :], [(hT[k][:, s * 128:(s + 1) * 128], v8(w_pool)[:, k, :]) for k in range(8)],
                            [w_pool] + hT)
                em.op("act", lambda e, s=s, b=b: e.activation(out=upool[s][:, :], in_=b[:, :], func=AF.Copy), [b], [upool[s]])
            release(1)
            for j in range(31):
                build_diag(0, j)
            for c in range(4):
                b = nb()
                for j0 in range(0, 31, 4):
                    js = list(range(j0, min(j0 + 4, 31)))
                    dgs = [diag[(dbase + c * 31 + j) % ND] for j in js]
                    em.mm_group(b, b[:, :], [(dg[:, :], aT[c][:, j:j + TT]) for dg, j in zip(dgs, js)], dgs + [aT[c]],
                                first=(j0 == 0), last=(js[-1] == 30))
                    if c < 3:
                        for j in js:
                            build_diag(c + 1, j)
                em.op("act", lambda e, c=c, b=b: e.activation(out=cacc[c][:, :], in_=b[:, :], func=AF.Identity,
                                                             bias=smT[l][:, c:c + 1]), [b, smT[l]], [cacc[c]])
            dctr[0] += 124
            bm, bq = nb(), nb()
            for c in range(4):
                yb_, yq_ = ybf[c % 2], ysq[c % 2]
                em.op("act", lambda e, c=c, yb_=yb_: e.activation(out=yb_[:, :], in_=cacc[c][:, :], func=AF.Copy), [cacc[c]], [yb_])
                em.op("act", lambda e, c=c, yq_=yq_: e.activation(out=yq_[:, :], in_=cacc[c][:, :], func=AF.Square), [cacc[c]], [yq_])
                need = em._deps("pe", [ones_b, yb_, yq_], [bm, bq])
                em._wait("pe", need)
                i1 = nc.tensor.matmul(bm[:, :], ones_b[:, :], yb_[:, :], start=(c == 0), stop=(c == 3))
                i2 = nc.tensor.matmul(bq[:, :], ones_b[:, :], yq_[:, :], start=(c == 0), stop=(c == 3))
                em._inc("pe", i2)
                em.ninst += 2
                dep = ("pe", em.sem["pe"], em.cnt["pe"], "pe")
                em._mark(dep, [ones_b, yb_, yq_], [bm, bq])
            em.op("act", lambda e: e.activation(out=lnm[:, :], in_=bm[:, :], func=AF.Copy), [bm], [lnm])
            em.op("dve", lambda e: e.tensor_tensor(out=lnt[:, :], in0=lnm[:, :], in1=lnm[:, :], op=ALU.mult), [lnm], [lnt])
            em.op("dve", lambda e: e.tensor_tensor(out=lnt[:, :], in0=bq[:, :], in1=lnt[:, :], op=ALU.subtract), [bq, lnt], [lnt])
            for g in range(4):
                b = nb()
                for s in range(4):
                    cur = upool[s][:, g * 128:(g + 1) * 128]
                    o = b[:, s * 128:(s + 1) * 128]
                    if s == 0 and first:
                        em.mm_group(b, o, [(cur, tm[:, 8 + g, :])], [upool[s], tm])
                    else:
                        pb = phalo[l] if s == 0 else upool[s - 1]
                        em.mm_group(b, o, [(cur, tm[:, g, :]), (pb[:, g * 128:(g + 1) * 128], tm[:, 4 + g, :])],
                                    [upool[s], pb, tm])
                dd = dT[g % 2]
                em.op("dve", lambda e, dd=dd, b=b: e.tensor_copy(out=dd[:, :], in_=b[:, :]), [b], [dd])
                b2 = nb()
                if g == 0:
                    em.dma("sp", poolw[:, :, :], s_pool[l].rearrange("g c d -> c g d"), [cvb[l]["pool"]], [poolw], poolw)
                em.mm_group(b2, b2[:, :], [(poolw[:, g, :], dd[:, :])], [poolw, dd])
                em.op("dve", lambda e, g=g, b2=b2: e.tensor_scalar(out=mT[4 + g][:, :], in0=b2[:, :],
                                                                   scalar1=smT[l][:, 12 + g:13 + g], scalar2=None, op0=ALU.mult),
                      [b2, smT[l]], [mT[4 + g]])
            em.op("pool", lambda e: e.tensor_copy(out=phalo[l][:, :], in_=upool[3][:, :]), [upool[3]], [phalo[l]])
            wo = [take(), take()]
            load_gg(l, seq)
            obs = []
            for s in range(4):
                ob = [nb(), nb()]
                for h in range(2):
                    em.mm_group(ob[h], ob[h][:, :],
                                [(mT[k][:, s * 128:(s + 1) * 128], v8(wo[h])[:, k, :]) for k in range(4, 8)], [wo[h]] + mT[4:8],
                                first=True, last=False)
                obs.append(ob)
            wo_banks = (wo, obs)
            em.op("act", lambda e: e.activation(out=lnr[:, :], in_=lnt[:, :], func=AF.Sqrt, bias=EPS), [lnt], [lnr])
            em.op("dve", lambda e: e.reciprocal(out=lnr[:, :], in_=lnr[:, :]), [lnr], [lnr])
            for c in range(4):
                z = zt[c % 2]
                em.op("dve", lambda e, c=c, z=z: e.tensor_tensor(out=z[:, :], in0=cacc[c][:, :], in1=lnm[:, :], op=ALU.subtract),
                      [cacc[c], lnm], [z])
                em.op("dve", lambda e, z=z: e.tensor_tensor(out=z[:, :], in0=z[:, :], in1=lnr[:, :], op=ALU.mult), [z, lnr], [z])
                em.op("act", lambda e, c=c, z=z: e.activation(out=mT[c][:, :], in_=z[:, :], func=AF.Silu,
                                                             scale=smT[l][:, 4 + c:5 + c], bias=smT[l][:, 8 + c:9 + c]),
                      [z, smT[l]], [mT[c]])
            return wo_banks

        def mixer_tail(i, l, wo, obs):
            g = ggb[0]
            for s in range(4):
                ob = obs[s]
                for h in range(2):
                    em.mm_group(ob[h], ob[h][:, :],
                                [(mT[k][:, s * 128:(s + 1) * 128], v8(wo[h])[:, k, :]) for k in range(4)], [wo[h]] + mT[0:4],
                                first=False, last=True)
            release(2)
            for s in range(4):
                for h in range(2):
                    em.op("act", lambda e, s=s, h=h: e.activation(out=xn[s][:, h * 512:(h + 1) * 512], in_=obs[s][h][:, :],
                                                                 func=AF.Square, accum_out=st4[:, 2 * s + h:2 * s + h + 1]),
                          [obs[s][h]], [xn[s], st4])
            em.op("dve", lambda e: e.tensor_tensor(out=st4[:, 8:12], in0=st4[:, 0:8:2], in1=st4[:, 1:8:2], op=ALU.add), [st4], [st4])
            em.op("act", lambda e: e.activation(out=st4[:, 12:16], in_=st4[:, 8:12], func=AF.Sqrt, scale=1.0 / D, bias=EPS),
                  [st4], [st4])
            em.op("dve", lambda e: e.reciprocal(out=st4[:, 12:16], in_=st4[:, 12:16]), [st4], [st4])
            for s in range(4):
                pt = ptmps[s % 2]
                for h in range(2):
                    em.op("dve", lambda e, s=s, h=h, pt=pt: e.scalar_tensor_tensor(
                        out=pt[:, h * 512:(h + 1) * 512], in0=obs[s][h][:, :], scalar=st4[:, 12 + s:13 + s],
                        in1=g[:, h * 512:(h + 1) * 512], op0=ALU.mult, op1=ALU.mult), [obs[s][h], st4, g], [pt])
                em.op("pool" if s < 3 else "dve",
                      lambda e, s=s, pt=pt: e.tensor_tensor(out=xs[s][:, :], in0=xs[s][:, :], in1=pt[:, :], op=ALU.add),
                      [xs[s], pt], [xs[s]])
            for s in range(4):
                pre_elem(s)

        def ffn(i, l):
            seq = i // tiles_per_seq
            first = (i % tiles_per_seq) == 0
            if i == 0 and l + 1 < L:
                convert_weights(l + 1)
            pre_finish(l, seq, 4, 3)
            if first:
                em.op("pool", lambda e: e.memset(fhalo[l][:, :, :], 0.0), [], [fhalo[l]])
            fh = fhalo[l]
            W0, W1 = fwT[l][:, 0:44], fwT[l][:, 44:88]
            em.op("pool", lambda e: e.tensor_tensor(out=fcorr[:, :, 0], in0=fh[:, :, 0], in1=W0, op=ALU.mult), [fh, fwT[l]], [fcorr])
            em.op("pool", lambda e: e.tensor_tensor(out=fcorr[:, :, 1], in0=fh[:, :, 1], in1=W0, op=ALU.mult), [fh, fwT[l]], [fcorr])
            em.op("pool", lambda e: e.tensor_tensor(out=fcorr2[:, :], in0=fh[:, :, 1], in1=W1, op=ALU.mult), [fh, fwT[l]], [fcorr2])
            em.op("pool", lambda e: e.tensor_tensor(out=fcorr[:, :, 0], in0=fcorr[:, :, 0], in1=fcorr2[:, :], op=ALU.add),
                  [fcorr, fcorr2], [fcorr])
            pieces = {}

            def stage_ab(j):
                q, jj = j // 4, j % 4
                n = 512 if q < 5 else 256
                if jj == 0:
                    pieces[q] = (take(), take())
                banks = []
                for gi, w in enumerate(pieces[q]):
                    b = nb()
                    if j == 0 and gi == 0:
                        for k in range(8):
                            em.mm_group(b, b[:, :], [(v8(w, n)[:, k, 0:128], hT[k][:, :])], [w, hT[k]],
                                        first=(k == 0), last=(k == 7))
                    else:
                        em.mm_group(b, b[:, :], [(v8(w, n)[:, k, jj * 128:(jj + 1) * 128], hT[k][:, :]) for k in range(8)],
                                    [w] + hT)
                    banks.append(b)
                if jj == n // 128 - 1:
                    release(2)
                for gi, b in enumerate(banks):
                    ch = gi * NF + j
                    acc = facc[(j % 4) * 2 + gi]
                    w2 = fwT[l][:, 2 * 44 + ch:2 * 44 + ch + 1]
                    em.op("act", lambda e, acc=acc, b=b, w2=w2, ch=ch: e.activation(out=acc[:, :], in_=b[:, :], func=AF.Identity,
                                                                                   scale=w2, bias=fbT[l][:, ch:ch + 1]),
                          [b, fwT[l], fbT[l]], [acc])
                    em.op("act", lambda e, ch=ch, b=b: e.activation(out=fh[:, ch, 0:2], in_=b[:, TT - 2:TT], func=AF.Copy), [b], [fh])
                for gi, b in enumerate(banks):
                    ch = gi * NF + j
                    acc = facc[(j % 4) * 2 + gi]
                    w0 = fwT[l][:, 0 * 44 + ch:0 * 44 + ch + 1]
                    w1 = fwT[l][:, 1 * 44 + ch:1 * 44 + ch + 1]
                    em.op("dve", lambda e, acc=acc, b=b, w1=w1: e.scalar_tensor_tensor(out=acc[:, 1:TT], in0=b[:, 0:TT - 1], scalar=w1,
                                                                                      in1=acc[:, 1:TT], op0=ALU.mult, op1=ALU.add),
                          [b, fwT[l], acc], [acc])
                    em.op("dve", lambda e, acc=acc, b=b, w0=w0: e.scalar_tensor_tensor(out=acc[:, 2:TT], in0=b[:, 0:TT - 2], scalar=w0,
                                                                                      in1=acc[:, 2:TT], op0=ALU.mult, op1=ALU.add),
                          [b, fwT[l], acc], [acc])
                for gi in range(2):
                    ch = gi * NF + j
                    acc = facc[(j % 4) * 2 + gi]
                    em.op("pool", lambda e, acc=acc, ch=ch: e.tensor_tensor(out=acc[:, 0:2], in0=acc[:, 0:2], in1=fcorr[:, ch, :],
                                                                            op=ALU.add), [acc, fcorr], [acc])

            def stage_d(j):
                av, ag = facc[(j % 4) * 2], facc[(j % 4) * 2 + 1]
                em.op("act", lambda e: e.activation(out=ag[:, :], in_=ag[:, :], func=AF.Silu), [ag], [ag])
                em.op("pool", lambda e: e.tensor_tensor(out=hid[j][:, :], in0=av[:, :], in1=ag[:, :], op=ALU.mult),
                      [av, ag], [hid[j]])

            for j in range(NF + 2):
                if j < NF:
                    stage_ab(j)
                if j >= 2:
                    stage_d(j - 2)
            wd = [take() for _ in range(6)]
            for s in range(4):
                ob = [nb(), nb()]
                for h in range(2):
                    pairs = []
                    for j in range(NF):
                        q, jj = j // 4, j % 4
                        nj = 4 if q < 5 else 2
                        pairs.append((hid[j][:, s * 128:(s + 1) * 128], v_down(wd[q], nj)[:, jj, h * 512:(h + 1) * 512]))
                    if s == 0 and h == 0:
                        em.mm_group(ob[h], ob[h][:, :], pairs[:16], wd + hid[:16], first=True, last=False)
                        em.mm_group(ob[h], ob[h][:, :], pairs[16:], wd + hid[16:], first=False, last=True)
                    else:
                        em.mm_group(ob[h], ob[h][:, :], pairs, wd + hid)
                if l < L - 1:
                    postnorm(5, s, ob, pool_add=(s < 3))
                    pre_elem(s)
                else:
                    r0 = i * TT + s * 128
                    postnorm(5, s, ob, final=(r0, r0 + TT if i + 1 < n_tiles else None))
                    if i + 1 < n_tiles:
                        pre_elem(s)
            release(6)

        for s in range(4):
            em.dma("sp", xs[s][:, :], x_d[s * 128:(s + 1) * 128, :], [], [xs[s]], xs[s])
            pre_elem(s)
        for i in range(n_tiles):
            for l in range(L):
                wo, obs = mixer(i, l)
                mixer_tail(i, l, wo, obs)
                ffn(i, l)
        for s in range(4):
            nc.sync.wait_ge(yb[s].dsem, yb[s].dcnt)
        if needed is not None:
            print("instructions:", em.ninst, "waits:", em.nwaits, "milestones:", em.cnt, "incs:", em.real)
    return nc, em.waited


def _consts():
    ident = np.eye(128, dtype=np.float32)
    tm = np.zeros((12, 128, 128), np.float32)
    tp = np.arange(128)[:, None]
    t = np.arange(128)[None, :]
    for g, w in enumerate(POOL_W):
        d = t - tp
        tm[g] = ((d >= 0) & (d < w)) / w - (d == 0)
        d2 = t + 128 - tp
        tm[4 + g] = ((d2 > 0) & (d2 < w)) / w
        cnt = np.minimum(t + 1, w).astype(np.float32)
        tm[8 + g] = ((d >= 0) & (d < w)) / cnt - (d == 0)
    return ident, tm


_NAMES = ["ada_w", "ada_b", "pre_mix_g", "post_mix_g", "w_in", "conv_w", "conv_b", "conv_ln_g", "conv_ln_b",
          "pool_w", "pool_scale", "w_out", "pre_ffn_g", "post_ffn_g", "ffn_up", "ffn_conv_w", "ffn_conv_b", "ffn_down"]


def run(inputs, n_cores, L, seq_per_core, seq_len, dbg=None):
    nc = build(L, seq_per_core, seq_len, dbg=dbg)
    ident, tm = _consts()
    x = np.ascontiguousarray(inputs["x"], dtype=np.float32)
    c = np.ascontiguousarray(inputs["c"], dtype=np.float32)
    in_maps = []
    for r in range(n_cores):
        m = {"x": x[r * seq_per_core:(r + 1) * seq_per_core].reshape(seq_per_core * seq_len, D),
             "c": c[r * seq_per_core:(r + 1) * seq_per_core], "ident": ident, "tmats": tm}
        for n in _NAMES:
            m[n] = np.ascontiguousarray(inputs[n][:L], dtype=np.float32)
        in_maps.append(m)
    res = run_bass_kernel_spmd(nc, in_maps, core_ids=list(range(n_cores)))
    y = np.concatenate([r["y"].reshape(seq_per_core, seq_len, D) for r in res.results], axis=0)
    if dbg is not None:
        return y, [r["dbg"] for r in res.results]
    return y


def kernel(**inputs):
    return run(inputs, 8, 4, 2, 2048).astype(np.float32)
```

```python
import contextlib
import numpy as np
import concourse.bass as bass
import concourse.mybir as mybir
from concourse.bass_utils import run_bass_kernel_spmd

F32 = mybir.dt.float32
BF16 = mybir.dt.bfloat16
AF = mybir.ActivationFunctionType
ALU = mybir.AluOpType

D = 1024
DC = 512
DFF = 2816
NF = 22
TT = 512
EPS = 1e-6
NSLOT = 10
POOL_W = (2, 4, 8, 16)


class Buf:
    def __init__(self, name, t, psum=False):
        self.name = name
        self.t = t
        self.psum = psum
        self.writer = None
        self.readers = []
        self.dsem = None
        self.dcnt = 0

    def __getitem__(self, k):
        return self.t[k]


class Em:
    def __init__(self, nc, stack, needed=None):
        self.nc = nc
        self.stack = stack
        self.needed = needed
        self.waited = set()
        self.real = {}
        self.realmap = {}
        self.eng = {"pe": nc.tensor, "act": nc.scalar, "dve": nc.vector, "pool": nc.gpsimd, "sp": nc.sync}
        self.sem = {k: stack.enter_context(nc.semaphore("sem_" + k)) for k in self.eng}
        self.cnt = {k: 0 for k in self.eng}
        self.real = {k: 0 for k in self.eng}
        self.seen = {k: {} for k in self.eng}
        self.nwaits = 0
        self.ninst = 0

    def sb(self, name, shape, dt):
        return Buf(name, self.nc.alloc_sbuf_tensor(name, list(shape), dt))

    def ps(self, name, shape, dt=F32):
        return Buf(name, self.nc.alloc_psum_tensor(name, list(shape), dt), psum=True)

    def _deps(self, engine, reads, writes):
        deps = []
        for b in reads:
            if b.writer is not None:
                deps.append((b.writer, True))
            if b.psum:
                deps += [(r, True) for r in b.readers]
        for b in writes:
            if b.writer is not None:
                deps.append((b.writer, b.psum))
            deps += [(r, b.psum) for r in b.readers]
        need = {}
        for (key, sem, val, src), raw in deps:
            if src == engine:
                if engine == "pe" or engine == "sp":
                    continue
                if not raw or val < self.cnt[engine]:
                    continue
            if key not in need or need[key][1] < val:
                need[key] = (sem, val)
        return need

    def _wait(self, engine, need):
        e = self.eng[engine]
        seen = self.seen[engine]
        for key, (sem, val) in need.items():
            if seen.get(key, 0) >= val:
                continue
            rv = val
            if key in self.eng:
                self.waited.add((key, val))
                if self.needed is not None:
                    rv = self.realmap[(key, val)]
            e.wait_ge(sem, rv)
            seen[key] = val
            self.nwaits += 1

    def _mark(self, dep, reads, writes):
        for b in reads:
            if b.psum:
                b.writer = dep
                b.readers = []
            else:
                b.readers = [r for r in b.readers if r[0] != dep[0]] + [dep]
        for b in writes:
            b.writer = dep
            b.readers = []

    def _inc(self, engine, inst):
        self.cnt[engine] += 1
        if self.needed is None or (engine, self.cnt[engine]) in self.needed:
            inst.then_inc(self.sem[engine], 1)
            self.real[engine] += 1
            self.realmap[(engine, self.cnt[engine])] = self.real[engine]

    def op(self, engine, fn, reads=(), writes=()):
        need = self._deps(engine, reads, writes)
        self._wait(engine, need)
        inst = fn(self.eng[engine])
        self._inc(engine, inst)
        self.ninst += 1
        dep = (engine, self.sem[engine], self.cnt[engine], engine)
        self._mark(dep, reads, writes)

    def multi(self, engine, fns, reads=(), writes=()):
        need = self._deps(engine, reads, writes)
        self._wait(engine, need)
        inst = None
        for fn in fns:
            inst = fn(self.eng[engine])
            self.ninst += 1
        self._inc(engine, inst)
        dep = (engine, self.sem[engine], self.cnt[engine], engine)
        self._mark(dep, reads, writes)

    def mm_group(self, out_buf, out_ap, pairs, reads, first=True, last=True):
        need = self._deps("pe", reads, [out_buf])
        self._wait("pe", need)
        n = len(pairs)
        inst = None
        for i, (l, r) in enumerate(pairs):
            inst = self.nc.tensor.matmul(out_ap, l, r, start=(first and i == 0), stop=(last and i == n - 1))
            self.ninst += 1
        self._inc("pe", inst)
        dep = ("pe", self.sem["pe"], self.cnt["pe"], "pe")
        self._mark(dep, reads, [out_buf])

    def dma(self, queue, out_ap, in_ap, reads, writes, owner):
        if owner.dsem is None:
            owner.dsem = self.stack.enter_context(self.nc.semaphore("dsem_" + owner.name))
        need = self._deps(queue, reads, writes)
        if owner.dcnt > 0:
            key = ("d", owner.name)
            if key not in need or need[key][1] < owner.dcnt:
                need[key] = (owner.dsem, owner.dcnt)
        self._wait(queue, need)
        inst = self.eng[queue].dma_start(out=out_ap, in_=in_ap)
        owner.dcnt += 16
        inst.then_inc(owner.dsem, 16)
        self.ninst += 1
        dep = (("d", owner.name), owner.dsem, owner.dcnt, None)
        self._mark(dep, reads, writes)
        return dep


def build(L, n_seq, seq_len, dbg=None):
    _, needed = _build(L, n_seq, seq_len, None)
    nc, _ = _build(L, n_seq, seq_len, needed)
    return nc


def _build(L, n_seq, seq_len, needed):
    P = 128
    dbg = None
    nc = bass.Bass("TRN2", target_bir_lowering=False)
    NTOK = n_seq * seq_len
    tiles_per_seq = seq_len // TT
    n_tiles = n_seq * tiles_per_seq

    def din(name, shape):
        return nc.dram_tensor(name, list(shape), F32, kind="ExternalInput").ap()

    x_d = din("x", [NTOK, D])
    c_d = din("c", [n_seq, D])
    ada_w = din("ada_w", [L, D, 6 * D])
    ada_b = din("ada_b", [L, 6 * D])
    pre_mix_g = din("pre_mix_g", [L, D])
    post_mix_g = din("post_mix_g", [L, D])
    w_in = din("w_in", [L, D, 3 * DC])
    conv_w = din("conv_w", [L, 31, DC])
    conv_b = din("conv_b", [L, DC])
    conv_ln_g = din("conv_ln_g", [L, DC])
    conv_ln_b = din("conv_ln_b", [L, DC])
    pool_w = din("pool_w", [L, 4, 128, 128])
    pool_scale = din("pool_scale", [L, DC])
    w_out = din("w_out", [L, D, D])
    pre_ffn_g = din("pre_ffn_g", [L, D])
    post_ffn_g = din("post_ffn_g", [L, D])
    ffn_up = din("ffn_up", [L, D, 2 * DFF])
    ffn_conv_w = din("ffn_conv_w", [L, 3, 2 * DFF])
    ffn_conv_b = din("ffn_conv_b", [L, 2 * DFF])
    ffn_down = din("ffn_down", [L, DFF, D])
    ident_d = din("ident", [P, P])
    tm_d = din("tmats", [12, P, P])
    y_d = nc.dram_tensor("y", [NTOK, D], F32, kind="ExternalOutput").ap()
    dbg_d = None
    if dbg is not None:
        dbg_d = nc.dram_tensor("dbg", [P, dbg], F32, kind="ExternalOutput").ap()

    def dscr(name, shape, dt=BF16):
        return nc.dram_tensor(name, list(shape), dt, kind="Internal").ap()

    s_in = dscr("s_in", [L, D, 3 * DC])
    s_out = dscr("s_out", [L, D, D])
    s_pool = dscr("s_pool", [L, 4, P, P])
    s_up = dscr("s_up", [L, D, 2 * DFF])
    s_down = dscr("s_down", [L, DFF, D])
    modscr = dscr("modscr", [L, 12, D], F32)

    with contextlib.ExitStack() as stack:
        em = Em(nc, stack, needed)

        xs = [em.sb(f"x{s}", [P, D], F32) for s in range(4)]
        hT = [em.sb(f"hT{k}", [P, TT], BF16) for k in range(8)]
        hid = [em.sb(f"hid{j}", [P, TT], BF16) for j in range(NF)]
        ring = [em.sb(f"ring{i}", [P, 4096], BF16) for i in range(NSLOT)]
        poolw = em.sb("poolw", [P, 4, P], BF16)
        xn = [em.sb(f"xn{i}", [P, D], BF16) for i in range(4)]
        import os
        JK = os.environ.get("K_JUNK", "0")
        junk = em.sb("junk", [P, D], BF16) if JK != "0" else None
        aT = [em.sb(f"aT{c}", [P, 30 + TT], BF16) for c in range(4)]
        ND = 32
        diag = [em.sb(f"diag{i}", [P, P], BF16) for i in range(ND)]
        cacc = [em.sb(f"cacc{c}", [P, TT], F32) for c in range(4)]
        upool = [em.sb(f"upool{s}", [P, DC], BF16) for s in range(4)]
        dT, ybf, ysq = hid[8:10], hid[10:12], hid[12:14]
        lnm = em.sb("lnm", [P, TT], F32)
        lnr = em.sb("lnr", [P, TT], F32)
        lnt = em.sb("lnt", [P, TT], F32)
        zt = [em.sb(f"zt{i}", [P, TT], F32) for i in range(2)]
        sig = zt
        mT = hid[0:8]
        ggb = [em.sb(f"ggb{g}", [P, D], F32) for g in range(2)]
        facc = cacc + zt + [lnt, lnr]
        ptmps = [em.sb(f"ptmp{i}", [P, D], F32) for i in range(2)]
        ptmp = ptmps[0]
        st4 = em.sb("st4", [P, 16], F32)
        stat = [em.sb(f"stat{i}", [P, 2], F32) for i in range(4)]
        pst = em.sb("pst", [P, 8], F32)
        fcorr = em.sb("fcorr", [P, 2 * NF, 2], F32)
        fcorr2 = em.sb("fcorr2", [P, 2 * NF], F32)
        ident_f = em.sb("ident_f", [P, P], F32)
        ident_b = em.sb("ident_b", [P, P], BF16)
        ones_b = em.sb("ones_b", [P, P], BF16)
        tm = em.sb("tm", [P, 12, P], BF16)
        stage = em.sb("stage", [P, P], F32)
        cT = em.sb("cT", [P, 16], BF16)
        colT = [em.sb(f"colT{l}", [P, 96], F32) for l in range(L)]
        cwT = [em.sb(f"cwT{l}", [P, 124], F32) for l in range(L)]
        smT = [em.sb(f"smT{l}", [P, 16], F32) for l in range(L)]
        fwT = [em.sb(f"fwT{l}", [P, 132], F32) for l in range(L)]
        fbT = [em.sb(f"fbT{l}", [P, 44], F32) for l in range(L)]
        ahalo = [[em.sb(f"ahalo{l}_{c}", [P, 30], BF16) for c in range(4)] for l in range(L)]
        phalo = [em.sb(f"phalo{l}", [P, DC], BF16) for l in range(L)]
        fhalo = [em.sb(f"fhalo{l}", [P, 2 * NF, 2], F32) for l in range(L)]
        trow, brow, grow = ptmp, ggb[0], ggb[1]
        bank = [em.ps(f"bank{i}", [P, 512], F32) for i in range(8)]
        if needed is not None:
            print("sbuf bytes remaining:", nc.sbuf_bytes_remaining)

        bank_rr = [0]
        dctr = [0]

        def nb():
            b = bank[bank_rr[0] % 8]
            bank_rr[0] += 1
            return b

        cvb = [{w: Buf(f"cv{l}_{w}", None) for w in ("in", "pool", "out", "up", "down")} for l in range(L)]
        modb = [Buf(f"modb{l}", None) for l in range(L)]
        yb = [Buf(f"yb{s}", None) for s in range(4)]

        def convert_weights(l):
            for w, dst, src in (("in", s_in[l], w_in[l]), ("pool", s_pool[l], pool_w[l]), ("out", s_out[l], w_out[l]),
                                ("up", s_up[l], ffn_up[l]), ("down", s_down[l], ffn_down[l])):
                em.dma("pool", dst, src, [], [cvb[l][w]], cvb[l][w])

        em.dma("sp", ident_f[:], ident_d, [], [ident_f], ident_f)
        em.op("dve", lambda e: e.tensor_copy(out=ident_b[:], in_=ident_f[:]), [ident_f], [ident_b])
        em.op("dve", lambda e: e.memset(ones_b[:], 1.0 / 512.0), [], [ones_b])
        for i in range(12):
            em.dma("sp", stage[:], tm_d[i], [], [stage], stage)
            em.op("dve", lambda e, i=i: e.tensor_copy(out=tm[:, i, :], in_=stage[:]), [stage], [tm])

        def transpose_rows(src_ap, nrows, dst_buf, dst_ap, evac_engine="dve", func=None):
            em.dma("sp", stage[0:nrows, :], src_ap, [], [stage], stage)
            b = nb()
            em.op("pe", lambda e: e.transpose(b[:, 0:nrows], stage[0:nrows, :], ident_f[0:nrows, 0:nrows]),
                  [stage, ident_f], [b])
            if func is None:
                em.op("dve", lambda e: e.tensor_copy(out=dst_ap, in_=b[:, 0:nrows]), [b], [dst_buf])
            else:
                em.op("act", lambda e: e.activation(out=dst_ap, in_=b[:, 0:nrows], func=func), [b], [dst_buf])

        transpose_rows(c_d.rearrange("b (k p) -> (b k) p", p=P), n_seq * 8, cT, cT[:, 0:n_seq * 8], func=AF.Silu)

        for l in range(L):
            transpose_rows(conv_w[l].rearrange("k (c p) -> (k c) p", p=P), 124, cwT[l], cwT[l][:, :])
            for i, v in enumerate((conv_b, conv_ln_g, conv_ln_b, pool_scale)):
                transpose_rows(v[l].rearrange("(c p) -> c p", p=P), 4, smT[l], smT[l][:, 4 * i:4 * i + 4])
            fw = ffn_conv_w[l].rearrange("k (j p) -> (k j) p", p=P)
            transpose_rows(fw[0:128], 128, fwT[l], fwT[l][:, 0:128])
            transpose_rows(fw[128:132], 4, fwT[l], fwT[l][:, 128:132])
            transpose_rows(ffn_conv_b[l].rearrange("(j p) -> j p", p=P), 44, fbT[l], fbT[l][:, :])

        slot_i = [0]

        def next_slot():
            s = ring[slot_i[0] % NSLOT]
            slot_i[0] += 1
            return s

        gains = {1: pre_mix_g, 2: post_mix_g, 4: pre_ffn_g, 5: post_ffn_g}
        for l in range(L):
            for v in range(6):
                halves = []
                for h in range(2):
                    sl = next_slot()
                    c0 = v * D + h * 512
                    em.dma("pool", sl[:, :].rearrange("p (k n) -> p k n", k=8),
                           ada_w[l][:, c0:c0 + 512].rearrange("(k p) n -> p k n", p=P), [], [sl], sl)
                    b = nb()
                    slv = sl[:, :].rearrange("p (k n) -> p k n", k=8)
                    cTv = cT[:, 0:n_seq * 8].rearrange("p (b k) -> p b k", k=8)
                    em.mm_group(b, b[0:n_seq, :], [(cTv[:, :, k], slv[:, k, :]) for k in range(8)], [cT, sl])
                    halves.append(b)
                em.dma("sp", brow[0:n_seq, :], ada_b[l:l + 1, v * D:(v + 1) * D].broadcast_to([n_seq, D]), [], [brow], brow)
                for h in range(2):
                    em.op("dve", lambda e, h=h: e.tensor_tensor(out=trow[0:n_seq, h * 512:(h + 1) * 512],
                                                                in0=halves[h][0:n_seq, :],
                                                                in1=brow[0:n_seq, h * 512:(h + 1) * 512], op=ALU.add),
                          [halves[h], brow], [trow])
                if v in gains:
                    em.dma("sp", grow[0:n_seq, :], gains[v][l:l + 1, :].broadcast_to([n_seq, D]), [], [grow], grow)
                    em.op("dve", lambda e: e.scalar_tensor_tensor(out=trow[0:n_seq, :], in0=trow[0:n_seq, :], scalar=1.0,
                                                                  in1=grow[0:n_seq, :], op0=ALU.add, op1=ALU.mult),
                          [trow, grow], [trow])
                em.dma("sp", modscr[l, 2 * v:2 * v + n_seq, :], trow[0:n_seq, :], [trow], [modb[l]], modb[l])
            em.dma("sp", stage[0:96, :], modscr[l].rearrange("q (c p) -> (q c) p", p=P), [modb[l]], [stage], stage)
            b = nb()
            em.op("pe", lambda e: e.transpose(b[:, 0:96], stage[0:96, :], ident_f[0:96, 0:96]), [stage, ident_f], [b])
            em.op("dve", lambda e, l=l: e.tensor_copy(out=colT[l][:, :], in_=b[:, 0:96]), [b], [colT[l]])

        convert_weights(0)

        def v8(sl, n=512):
            return sl[:, 0:8 * n].rearrange("p (k n) -> p k n", k=8)

        def v_down(sl, nj):
            return sl[:, 0:nj * D].rearrange("p (j n) -> p j n", j=nj)

        sched = []
        for i in range(n_tiles):
            for l in range(L):
                for q in range(3):
                    sched.append((cvb[l]["in"], [(lambda sl: v8(sl),
                                       s_in[l][:, q * 512:(q + 1) * 512].rearrange("(k p) n -> p k n", p=P))]))
                for q in range(2):
                    sched.append((cvb[l]["out"], [(lambda sl: v8(sl),
                                       s_out[l][:, q * 512:(q + 1) * 512].rearrange("(k p) n -> p k n", p=P))]))
                for q in range(6):
                    n = 512 if q < 5 else 256
                    for g in range(2):
                        c0 = g * DFF + q * 512
                        sched.append((cvb[l]["up"], [(lambda sl, n=n: v8(sl, n),
                                           s_up[l][:, c0:c0 + n].rearrange("(k p) n -> p k n", p=P))]))
                for q in range(6):
                    nj = 4 if q < 5 else 2
                    sched.append((cvb[l]["down"], [(lambda sl, nj=nj: v_down(sl, nj),
                                       s_down[l][q * 512:q * 512 + nj * P, :].rearrange("(j p) n -> p j n", p=P))]))
        issued = [0]
        consumed = [0]
        released = [0]
        base_slot = slot_i[0]

        def pump():
            while issued[0] < len(sched) and issued[0] < released[0] + NSLOT:
                q = issued[0]
                cv, dmas = sched[q]
                sl = ring[(base_slot + q) % NSLOT]
                for dst_fn, src in dmas:
                    em.dma("sp", dst_fn(sl), src, [cv], [sl], sl)
                issued[0] += 1

        def take():
            p = consumed[0]
            pump()
            assert issued[0] > p, "ring too small for the pieces held at once"
            consumed[0] += 1
            return ring[(base_slot + p) % NSLOT]

        def release(n):
            released[0] += n
            assert released[0] <= consumed[0]
            pump()

        def pre_elem(s):
            xb = xn[s]
            jb = junk if JK in ("1", "3") else xb
            em.op("act", lambda e: e.activation(out=jb[:, :], in_=xs[s][:, :], func=AF.Square, accum_out=pst[:, s:s + 1]),
                  [xs[s]], [jb, pst])
            em.op("act", lambda e: e.activation(out=pst[:, 4 + s:5 + s], in_=pst[:, s:s + 1], func=AF.Sqrt, scale=1.0 / D, bias=EPS),
                  [pst], [pst])
            em.op("dve", lambda e: e.reciprocal(out=pst[:, 4 + s:5 + s], in_=pst[:, 4 + s:5 + s]), [pst], [pst])
            em.op("dve", lambda e: e.tensor_scalar(out=xb[:, :], in0=xs[s][:, :], scalar1=pst[:, 4 + s:5 + s], scalar2=None,
                                                   op0=ALU.mult), [xs[s], pst], [xb])

        def pre_finish(l, seq, vsc, vsh):
            ca = (vsc * 2 + seq) * 8
            cb = (vsh * 2 + seq) * 8
            tb = [nb(), nb(), nb(), nb()]
            for s in range(4):
                xb = xn[s]
                fns = []
                for k in range(8):
                    bv = tb[k // 2][:, :].bitcast(BF16)
                    o = bv[:, (k % 2) * 512 + s * 128:(k % 2) * 512 + (s + 1) * 128]
                    fns.append(lambda e, o=o, xb=xb, k=k: e.transpose(o, xb[:, k * 128:(k + 1) * 128], ident_b[:, :]))
                em.multi("pe", fns, [xb, ident_b], tb)
            for k in range(8):
                b = tb[k // 2]
                bv = b[:, :].bitcast(BF16)
                src = bv[:, (k % 2) * 512:(k % 2) * 512 + 512]
                if k % 2 == 0:
                    em.op("act", lambda e, k=k, src=src: e.activation(out=hT[k][:, :], in_=src, func=AF.Identity,
                                                                     scale=colT[l][:, ca + k:ca + k + 1],
                                                                     bias=colT[l][:, cb + k:cb + k + 1]), [b, colT[l]], [hT[k]])
                else:
                    em.op("dve", lambda e, k=k, src=src: e.tensor_scalar(out=hT[k][:, :], in0=src, scalar1=colT[l][:, ca + k:ca + k + 1],
                                                                        scalar2=colT[l][:, cb + k:cb + k + 1], op0=ALU.mult,
                                                                        op1=ALU.add), [b, colT[l]], [hT[k]])

        def postnorm(vg, s, ob, final=None, pool_add=False):
            st = stat[s]
            g = ggb[0 if vg == 2 else 1]
            for h in range(2):
                jb2 = junk if JK in ("2", "3") else ptmp
                em.op("act", lambda e, h=h: e.activation(out=jb2[:, h * 512:(h + 1) * 512], in_=ob[h][:, :], func=AF.Square,
                                                         accum_out=st[:, h:h + 1]), [ob[h]], [jb2, st])
            em.op("dve", lambda e: e.tensor_tensor(out=st[:, 0:1], in0=st[:, 0:1], in1=st[:, 1:2], op=ALU.add), [st], [st])
            em.op("act", lambda e: e.activation(out=st[:, 1:2], in_=st[:, 0:1], func=AF.Sqrt, scale=1.0 / D, bias=EPS), [st], [st])
            em.op("dve", lambda e: e.reciprocal(out=st[:, 1:2], in_=st[:, 1:2]), [st], [st])
            for h in range(2):
                em.op("dve", lambda e, h=h: e.scalar_tensor_tensor(out=ptmp[:, h * 512:(h + 1) * 512], in0=ob[h][:, :],
                                                                   scalar=st[:, 1:2], in1=g[:, h * 512:(h + 1) * 512],
                                                                   op0=ALU.mult, op1=ALU.mult), [ob[h], st, g], [ptmp])
            if final is None:
                em.op("pool" if pool_add else "dve",
                      lambda e: e.tensor_tensor(out=xs[s][:, :], in0=xs[s][:, :], in1=ptmp[:, :], op=ALU.add),
                      [xs[s], ptmp], [xs[s]])
            else:
                r0, nr0 = final
                em.op("dve", lambda e: e.tensor_tensor(out=ptmp[:, :], in0=xs[s][:, :], in1=ptmp[:, :], op=ALU.add),
                      [xs[s], ptmp], [ptmp])
                em.dma("sp", y_d[r0:r0 + 128, :], ptmp[:, :], [ptmp], [yb[s]], yb[s])
                if nr0 is not None:
                    em.dma("sp", xs[s][:, :], x_d[nr0:nr0 + 128, :], [], [xs[s]], xs[s])

        def load_gg(l, seq):
            for gi, v in enumerate((2, 5)):
                em.dma("sp", ggb[gi][:, :], modscr[l, 2 * v + seq:2 * v + seq + 1, :].broadcast_to([P, D]),
                       [modb[l]], [ggb[gi]], ggb[gi])

        def mixer(i, l):
            seq = i // tiles_per_seq
            first = (i % tiles_per_seq) == 0
            dbase = dctr[0]

            def build_diag(c, j):
                dg = diag[(dbase + c * 31 + j) % ND]
                w = cwT[l][:, j * 4 + c:j * 4 + c + 1]
                if j % 2 == 0:
                    em.op("dve", lambda e: e.tensor_scalar(out=dg[:, :], in0=ident_b[:, :], scalar1=w, scalar2=None, op0=ALU.mult),
                          [ident_b, cwT[l]], [dg])
                else:
                    em.op("act", lambda e: e.activation(out=dg[:, :], in_=ident_b[:, :], func=AF.Copy, scale=w),
                          [ident_b, cwT[l]], [dg])

            pre_finish(l, seq, 1, 0)
            w_val, w_gate, w_pool = take(), take(), take()
            for c in range(4):
                if first:
                    em.op("pool", lambda e, c=c: e.memset(aT[c][:, 0:30], 0.0), [], [aT[c]])
                else:
                    em.op("pool", lambda e, c=c: e.tensor_copy(out=aT[c][:, 0:30], in_=ahalo[l][c][:, :]),
                          [ahalo[l][c]], [aT[c]])
                bv, bg = nb(), nb()
                if c == 0:
                    for k in range(8):
                        em.mm_group(bv, bv[:, :], [(v8(w_val)[:, k, 0:128], hT[k][:, :])], [w_val, hT[k]],
                                    first=(k == 0), last=(k == 7))
                else:
                    em.mm_group(bv, bv[:, :], [(v8(w_val)[:, k, c * 128:(c + 1) * 128], hT[k][:, :]) for k in range(8)],
                                [w_val] + hT)
                em.mm_group(bg, bg[:, :], [(v8(w_gate)[:, k, c * 128:(c + 1) * 128], hT[k][:, :]) for k in range(8)],
                            [w_gate] + hT)
                sg = sig[c % 2]
                em.op("act", lambda e, sg=sg, bg=bg: e.activation(out=sg[:, :], in_=bg[:, :], func=AF.Sigmoid), [bg], [sg])
                em.op("dve", lambda e, c=c, sg=sg, bv=bv: e.tensor_tensor(out=aT[c][:, 30:30 + TT], in0=bv[:, :], in1=sg[:, :],
                                                                          op=ALU.mult), [bv, sg], [aT[c]])
                em.op("pool", lambda e, c=c: e.tensor_copy(out=ahalo[l][c][:, :], in_=aT[c][:, TT:TT + 30]),
                      [aT[c]], [ahalo[l][c]])
            release(2)
            for s in range(4):
                b = nb()
                em.mm_group(b, b[:, :], [(hT[k][:, s * 128:(s + 1) * 128], v8(w_pool)[:, k, :]) for k in range(8)],
                            [w_pool] + hT)
                em.op("act", lambda e, s=s, b=b: e.activation(out=upool[s][:, :], in_=b[:, :], func=AF.Copy), [b], [upool[s]])
            release(1)
            for j in range(31):
                build_diag(0, j)
            for c in range(4):
                b = nb()
                for j0 in range(0, 31, 4):
                    js = list(range(j0, min(j0 + 4, 31)))
                    dgs = [diag[(dbase + c * 31 + j) % ND] for j in js]
                    em.mm_group(b, b[:, :], [(dg[:, :], aT[c][:, j:j + TT]) for dg, j in zip(dgs, js)], dgs + [aT[c]],
                                first=(j0 == 0), last=(js[-1] == 30))
                    if c < 3:
                        for j in js:
                            build_diag(c + 1, j)
                em.op("act", lambda e, c=c, b=b: e.activation(out=cacc[c][:, :], in_=b[:, :], func=AF.Identity,
                                                             bias=smT[l][:, c:c + 1]), [b, smT[l]], [cacc[c]])
            dctr[0] += 124
            bm, bq = nb(), nb()
            for c in range(4):
                yb_, yq_ = ybf[c % 2], ysq[c % 2]
                em.op("act", lambda e, c=c, yb_=yb_: e.activation(out=yb_[:, :], in_=cacc[c][:, :], func=AF.Copy), [cacc[c]], [yb_])
                em.op("act", lambda e, c=c, yq_=yq_: e.activation(out=yq_[:, :], in_=cacc[c][:, :], func=AF.Square), [cacc[c]], [yq_])
                need = em._deps("pe", [ones_b, yb_, yq_], [bm, bq])
                em._wait("pe", need)
                i1 = nc.tensor.matmul(bm[:, :], ones_b[:, :], yb_[:, :], start=(c == 0), stop=(c == 3))
                i2 = nc.tensor.matmul(bq[:, :], ones_b[:, :], yq_[:, :], start=(c == 0), stop=(c == 3))
                em._inc("pe", i2)
                em.ninst += 2
                dep = ("pe", em.sem["pe"], em.cnt["pe"], "pe")
                em._mark(dep, [ones_b, yb_, yq_], [bm, bq])
            em.op("act", lambda e: e.activation(out=lnm[:, :], in_=bm[:, :], func=AF.Copy), [bm], [lnm])
            em.op("dve", lambda e: e.tensor_tensor(out=lnt[:, :], in0=lnm[:, :], in1=lnm[:, :], op=ALU.mult), [lnm], [lnt])
            em.op("dve", lambda e: e.tensor_tensor(out=lnt[:, :], in0=bq[:, :], in1=lnt[:, :], op=ALU.subtract), [bq, lnt], [lnt])
            for g in range(4):
                b = nb()
                for s in range(4):
                    cur = upool[s][:, g * 128:(g + 1) * 128]
                    o = b[:, s * 128:(s + 1) * 128]
                    if s == 0 and first:
                        em.mm_group(b, o, [(cur, tm[:, 8 + g, :])], [upool[s], tm])
                    else:
                        pb = phalo[l] if s == 0 else upool[s - 1]
                        em.mm_group(b, o, [(cur, tm[:, g, :]), (pb[:, g * 128:(g + 1) * 128], tm[:, 4 + g, :])],
                                    [upool[s], pb, tm])
                dd = dT[g % 2]
                em.op("dve", lambda e, dd=dd, b=b: e.tensor_copy(out=dd[:, :], in_=b[:, :]), [b], [dd])
                b2 = nb()
                if g == 0:
                    em.dma("sp", poolw[:, :, :], s_pool[l].rearrange("g c d -> c g d"), [cvb[l]["pool"]], [poolw], poolw)
                em.mm_group(b2, b2[:, :], [(poolw[:, g, :], dd[:, :])], [poolw, dd])
                em.op("dve", lambda e, g=g, b2=b2: e.tensor_scalar(out=mT[4 + g][:, :], in0=b2[:, :],
                                                                   scalar1=smT[l][:, 12 + g:13 + g], scalar2=None, op0=ALU.mult),
                      [b2, smT[l]], [mT[4 + g]])
            em.op("pool", lambda e: e.tensor_copy(out=phalo[l][:, :], in_=upool[3][:, :]), [upool[3]], [phalo[l]])
            wo = [take(), take()]
            load_gg(l, seq)
            obs = []
            for s in range(4):
                ob = [nb(), nb()]
                for h in range(2):
                    em.mm_group(ob[h], ob[h][:, :],
                                [(mT[k][:, s * 128:(s + 1) * 128], v8(wo[h])[:, k, :]) for k in range(4, 8)], [wo[h]] + mT[4:8],
                                first=True, last=False)
                obs.append(ob)
            wo_banks = (wo, obs)
            em.op("act", lambda e: e.activation(out=lnr[:, :], in_=lnt[:, :], func=AF.Sqrt, bias=EPS), [lnt], [lnr])
            em.op("dve", lambda e: e.reciprocal(out=lnr[:, :], in_=lnr[:, :]), [lnr], [lnr])
            for c in range(4):
                z = zt[c % 2]
                em.op("dve", lambda e, c=c, z=z: e.tensor_tensor(out=z[:, :], in0=cacc[c][:, :], in1=lnm[:, :], op=ALU.subtract),
                      [cacc[c], lnm], [z])
                em.op("dve", lambda e, z=z: e.tensor_tensor(out=z[:, :], in0=z[:, :], in1=lnr[:, :], op=ALU.mult), [z, lnr], [z])
                em.op("act", lambda e, c=c, z=z: e.activation(out=mT[c][:, :], in_=z[:, :], func=AF.Silu,
                                                             scale=smT[l][:, 4 + c:5 + c], bias=smT[l][:, 8 + c:9 + c]),
                      [z, smT[l]], [mT[c]])
            return wo_banks

        def mixer_tail(i, l, wo, obs):
            g = ggb[0]
            for s in range(4):
                ob = obs[s]
                for h in range(2):
                    em.mm_group(ob[h], ob[h][:, :],
                                [(mT[k][:, s * 128:(s + 1) * 128], v8(wo[h])[:, k, :]) for k in range(4)], [wo[h]] + mT[0:4],
                                first=False, last=True)
            release(2)
            for s in range(4):
                for h in range(2):
                    em.op("act", lambda e, s=s, h=h: e.activation(out=xn[s][:, h * 512:(h + 1) * 512], in_=obs[s][h][:, :],
                                                                 func=AF.Square, accum_out=st4[:, 2 * s + h:2 * s + h + 1]),
                          [obs[s][h]], [xn[s], st4])
            em.op("dve", lambda e: e.tensor_tensor(out=st4[:, 8:12], in0=st4[:, 0:8:2], in1=st4[:, 1:8:2], op=ALU.add), [st4], [st4])
            em.op("act", lambda e: e.activation(out=st4[:, 12:16], in_=st4[:, 8:12], func=AF.Sqrt, scale=1.0 / D, bias=EPS),
                  [st4], [st4])
            em.op("dve", lambda e: e.reciprocal(out=st4[:, 12:16], in_=st4[:, 12:16]), [st4], [st4])
            for s in range(4):
                pt = ptmps[s % 2]
                for h in range(2):
                    em.op("dve", lambda e, s=s, h=h, pt=pt: e.scalar_tensor_tensor(
                        out=pt[:, h * 512:(h + 1) * 512], in0=obs[s][h][:, :], scalar=st4[:, 12 + s:13 + s],
                        in1=g[:, h * 512:(h + 1) * 512], op0=ALU.mult, op1=ALU.mult), [obs[s][h], st4, g], [pt])
                em.op("pool" if s < 3 else "dve",
                      lambda e, s=s, pt=pt: e.tensor_tensor(out=xs[s][:, :], in0=xs[s][:, :], in1=pt[:, :], op=ALU.add),
                      [xs[s], pt], [xs[s]])
            for s in range(4):
                pre_elem(s)

        def ffn(i, l):
            seq = i // tiles_per_seq
            first = (i % tiles_per_seq) == 0
            if i == 0 and l + 1 < L:
                convert_weights(l + 1)
            pre_finish(l, seq, 4, 3)
            if first:
                em.op("pool", lambda e: e.memset(fhalo[l][:, :, :], 0.0), [], [fhalo[l]])
            fh = fhalo[l]
            W0, W1 = fwT[l][:, 0:44], fwT[l][:, 44:88]
            em.op("pool", lambda e: e.tensor_tensor(out=fcorr[:, :, 0], in0=fh[:, :, 0], in1=W0, op=ALU.mult), [fh, fwT[l]], [fcorr])
            em.op("pool", lambda e: e.tensor_tensor(out=fcorr[:, :, 1], in0=fh[:, :, 1], in1=W0, op=ALU.mult), [fh, fwT[l]], [fcorr])
            em.op("pool", lambda e: e.tensor_tensor(out=fcorr2[:, :], in0=fh[:, :, 1], in1=W1, op=ALU.mult), [fh, fwT[l]], [fcorr2])
            em.op("pool", lambda e: e.tensor_tensor(out=fcorr[:, :, 0], in0=fcorr[:, :, 0], in1=fcorr2[:, :], op=ALU.add),
                  [fcorr, fcorr2], [fcorr])
            pieces = {}

            def stage_ab(j):
                q, jj = j // 4, j % 4
                n = 512 if q < 5 else 256
                if jj == 0:
                    pieces[q] = (take(), take())
                banks = []
                for gi, w in enumerate(pieces[q]):
                    b = nb()
                    if j == 0 and gi == 0:
                        for k in range(8):
                            em.mm_group(b, b[:, :], [(v8(w, n)[:, k, 0:128], hT[k][:, :])], [w, hT[k]],
                                        first=(k == 0), last=(k == 7))
                    else:
                        em.mm_group(b, b[:, :], [(v8(w, n)[:, k, jj * 128:(jj + 1) * 128], hT[k][:, :]) for k in range(8)],
                                    [w] + hT)
                    banks.append(b)
                if jj == n // 128 - 1:
                    release(2)
                for gi, b in enumerate(banks):
                    ch = gi * NF + j
                    acc = facc[(j % 4) * 2 + gi]
                    w2 = fwT[l][:, 2 * 44 + ch:2 * 44 + ch + 1]
                    em.op("act", lambda e, acc=acc, b=b, w2=w2, ch=ch: e.activation(out=acc[:, :], in_=b[:, :], func=AF.Identity,
                                                                                   scale=w2, bias=fbT[l][:, ch:ch + 1]),
                          [b, fwT[l], fbT[l]], [acc])
                    em.op("act", lambda e, ch=ch, b=b: e.activation(out=fh[:, ch, 0:2], in_=b[:, TT - 2:TT], func=AF.Copy), [b], [fh])
                for gi, b in enumerate(banks):
                    ch = gi * NF + j
                    acc = facc[(j % 4) * 2 + gi]
                    w0 = fwT[l][:, 0 * 44 + ch:0 * 44 + ch + 1]
                    w1 = fwT[l][:, 1 * 44 + ch:1 * 44 + ch + 1]
                    em.op("dve", lambda e, acc=acc, b=b, w1=w1: e.scalar_tensor_tensor(out=acc[:, 1:TT], in0=b[:, 0:TT - 1], scalar=w1,
                                                                                      in1=acc[:, 1:TT], op0=ALU.mult, op1=ALU.add),
                          [b, fwT[l], acc], [acc])
                    em.op("dve", lambda e, acc=acc, b=b, w0=w0: e.scalar_tensor_tensor(out=acc[:, 2:TT], in0=b[:, 0:TT - 2], scalar=w0,
                                                                                      in1=acc[:, 2:TT], op0=ALU.mult, op1=ALU.add),
                          [b, fwT[l], acc], [acc])
                for gi in range(2):
                    ch = gi * NF + j
                    acc = facc[(j % 4) * 2 + gi]
                    em.op("pool", lambda e, acc=acc, ch=ch: e.tensor_tensor(out=acc[:, 0:2], in0=acc[:, 0:2], in1=fcorr[:, ch, :],
                                                                            op=ALU.add), [acc, fcorr], [acc])

            def stage_d(j):
                av, ag = facc[(j % 4) * 2], facc[(j % 4) * 2 + 1]
                em.op("act", lambda e: e.activation(out=ag[:, :], in_=ag[:, :], func=AF.Silu), [ag], [ag])
                em.op("pool", lambda e: e.tensor_tensor(out=hid[j][:, :], in0=av[:, :], in1=ag[:, :], op=ALU.mult),
                      [av, ag], [hid[j]])

            for j in range(NF + 2):
                if j < NF:
                    stage_ab(j)
                if j >= 2:
                    stage_d(j - 2)
            wd = [take() for _ in range(6)]
            for s in range(4):
                ob = [nb(), nb()]
                for h in range(2):
                    pairs = []
                    for j in range(NF):
                        q, jj = j // 4, j % 4
                        nj = 4 if q < 5 else 2
                        pairs.append((hid[j][:, s * 128:(s + 1) * 128], v_down(wd[q], nj)[:, jj, h * 512:(h + 1) * 512]))
                    if s == 0 and h == 0:
                        em.mm_group(ob[h], ob[h][:, :], pairs[:16], wd + hid[:16], first=True, last=False)
                        em.mm_group(ob[h], ob[h][:, :], pairs[16:], wd + hid[16:], first=False, last=True)
                    else:
                        em.mm_group(ob[h], ob[h][:, :], pairs, wd + hid)
                if l < L - 1:
                    postnorm(5, s, ob, pool_add=(s < 3))
                    pre_elem(s)
                else:
                    r0 = i * TT + s * 128
                    postnorm(5, s, ob, final=(r0, r0 + TT if i + 1 < n_tiles else None))
                    if i + 1 < n_tiles:
                        pre_elem(s)
            release(6)

        for s in range(4):
            em.dma("sp", xs[s][:, :], x_d[s * 128:(s + 1) * 128, :], [], [xs[s]], xs[s])
            pre_elem(s)
        for i in range(n_tiles):
            for l in range(L):
                wo, obs = mixer(i, l)
                mixer_tail(i, l, wo, obs)
                ffn(i, l)
        for s in range(4):
            nc.sync.wait_ge(yb[s].dsem, yb[s].dcnt)
        if needed is not None:
            print("instructions:", em.ninst, "waits:", em.nwaits, "milestones:", em.cnt, "incs:", em.real)
    return nc, em.waited


def _consts():
    ident = np.eye(128, dtype=np.float32)
    tm = np.zeros((12, 128, 128), np.float32)
    tp = np.arange(128)[:, None]
    t = np.arange(128)[None, :]
    for g, w in enumerate(POOL_W):
        d = t - tp
        tm[g] = ((d >= 0) & (d < w)) / w - (d == 0)
        d2 = t + 128 - tp
        tm[4 + g] = ((d2 > 0) & (d2 < w)) / w
        cnt = np.minimum(t + 1, w).astype(np.float32)
        tm[8 + g] = ((d >= 0) & (d < w)) / cnt - (d == 0)
    return ident, tm


_NAMES = ["ada_w", "ada_b", "pre_mix_g", "post_mix_g", "w_in", "conv_w", "conv_b", "conv_ln_g", "conv_ln_b",
          "pool_w", "pool_scale", "w_out", "pre_ffn_g", "post_ffn_g", "ffn_up", "ffn_conv_w", "ffn_conv_b", "ffn_down"]


def run(inputs, n_cores, L, seq_per_core, seq_len, dbg=None):
    nc = build(L, seq_per_core, seq_len, dbg=dbg)
    ident, tm = _consts()
    x = np.ascontiguousarray(inputs["x"], dtype=np.float32)
    c = np.ascontiguousarray(inputs["c"], dtype=np.float32)
    in_maps = []
    for r in range(n_cores):
        m = {"x": x[r * seq_per_core:(r + 1) * seq_per_core].reshape(seq_per_core * seq_len, D),
             "c": c[r * seq_per_core:(r + 1) * seq_per_core], "ident": ident, "tmats": tm}
        for n in _NAMES:
            m[n] = np.ascontiguousarray(inputs[n][:L], dtype=np.float32)
        in_maps.append(m)
    res = run_bass_kernel_spmd(nc, in_maps, core_ids=list(range(n_cores)))
    y = np.concatenate([r["y"].reshape(seq_per_core, seq_len, D) for r in res.results], axis=0)
    if dbg is not None:
        return y, [r["dbg"] for r in res.results]
    return y


def kernel(**inputs):
    return run(inputs, 8, 4, 2, 2048).astype(np.float32)
```

```python
import contextlib
import numpy as np
import concourse.bass as bass
import concourse.mybir as mybir
from concourse.bass_utils import run_bass_kernel_spmd

F32 = mybir.dt.float32
BF16 = mybir.dt.bfloat16
AF = mybir.ActivationFunctionType
ALU = mybir.AluOpType

D = 1024
DC = 512
DFF = 2816
NF = 22
TT = 512
EPS = 1e-6
NSLOT = 10
POOL_W = (2, 4, 8, 16)


class Buf:
    def __init__(self, name, t, psum=False):
        self.name = name
        self.t = t
        self.psum = psum
        self.writer = None
        self.readers = []
        self.dsem = None
        self.dcnt = 0

    def __getitem__(self, k):
        return self.t[k]


class Em:
    def __init__(self, nc, stack, needed=None):
        self.nc = nc
        self.stack = stack
        self.needed = needed
        self.waited = set()
        self.real = {}
        self.realmap = {}
        self.eng = {"pe": nc.tensor, "act": nc.scalar, "dve": nc.vector, "pool": nc.gpsimd, "sp": nc.sync}
        self.sem = {k: stack.enter_context(nc.semaphore("sem_" + k)) for k in self.eng}
        self.cnt = {k: 0 for k in self.eng}
        self.real = {k: 0 for k in self.eng}
        self.seen = {k: {} for k in self.eng}
        self.nwaits = 0
        self.ninst = 0

    def sb(self, name, shape, dt):
        return Buf(name, self.nc.alloc_sbuf_tensor(name, list(shape), dt))

    def ps(self, name, shape, dt=F32):
        return Buf(name, self.nc.alloc_psum_tensor(name, list(shape), dt), psum=True)

    def _deps(self, engine, reads, writes):
        deps = []
        for b in reads:
            if b.writer is not None:
                deps.append((b.writer, True))
            if b.psum:
                deps += [(r, True) for r in b.readers]
        for b in writes:
            if b.writer is not None:
                deps.append((b.writer, b.psum))
            deps += [(r, b.psum) for r in b.readers]
        need = {}
        for (key, sem, val, src), raw in deps:
            if src == engine:
                if engine == "pe" or engine == "sp":
                    continue
                if not raw or val < self.cnt[engine]:
                    continue
            if key not in need or need[key][1] < val:
                need[key] = (sem, val)
        return need

    def _wait(self, engine, need):
        e = self.eng[engine]
        seen = self.seen[engine]
        for key, (sem, val) in need.items():
            if seen.get(key, 0) >= val:
                continue
            rv = val
            if key in self.eng:
                self.waited.add((key, val))
                if self.needed is not None:
                    rv = self.realmap[(key, val)]
            e.wait_ge(sem, rv)
            seen[key] = val
            self.nwaits += 1

    def _mark(self, dep, reads, writes):
        for b in reads:
            if b.psum:
                b.writer = dep
                b.readers = []
            else:
                b.readers = [r for r in b.readers if r[0] != dep[0]] + [dep]
        for b in writes:
            b.writer = dep
            b.readers = []

    def _inc(self, engine, inst):
        self.cnt[engine] += 1
        if self.needed is None or (engine, self.cnt[engine]) in self.needed:
            inst.then_inc(self.sem[engine], 1)
            self.real[engine] += 1
            self.realmap[(engine, self.cnt[engine])] = self.real[engine]

    def op(self, engine, fn, reads=(), writes=()):
        need = self._deps(engine, reads, writes)
        self._wait(engine, need)
        inst = fn(self.eng[engine])
        self._inc(engine, inst)
        self.ninst += 1
        dep = (engine, self.sem[engine], self.cnt[engine], engine)
        self._mark(dep, reads, writes)

    def multi(self, engine, fns, reads=(), writes=()):
        need = self._deps(engine, reads, writes)
        self._wait(engine, need)
        inst = None
        for fn in fns:
            inst = fn(self.eng[engine])
            self.ninst += 1
        self._inc(engine, inst)
        dep = (engine, self.sem[engine], self.cnt[engine], engine)
        self._mark(dep, reads, writes)

    def mm_group(self, out_buf, out_ap, pairs, reads, first=True, last=True):
        need = self._deps("pe", reads, [out_buf])
        self._wait("pe", need)
        n = len(pairs)
        inst = None
        for i, (l, r) in enumerate(pairs):
            inst = self.nc.tensor.matmul(out_ap, l, r, start=(first and i == 0), stop=(last and i == n - 1))
            self.ninst += 1
        self._inc("pe", inst)
        dep = ("pe", self.sem["pe"], self.cnt["pe"], "pe")
        self._mark(dep, reads, [out_buf])

    def dma(self, queue, out_ap, in_ap, reads, writes, owner):
        if owner.dsem is None:
            owner.dsem = self.stack.enter_context(self.nc.semaphore("dsem_" + owner.name))
        need = self._deps(queue, reads, writes)
        if owner.dcnt > 0:
            key = ("d", owner.name)
            if key not in need or need[key][1] < owner.dcnt:
                need[key] = (owner.dsem, owner.dcnt)
        self._wait(queue, need)
        inst = self.eng[queue].dma_start(out=out_ap, in_=in_ap)
        owner.dcnt += 16
        inst.then_inc(owner.dsem, 16)
        self.ninst += 1
        dep = (("d", owner.name), owner.dsem, owner.dcnt, None)
        self._mark(dep, reads, writes)
        return dep


def build(L, n_seq, seq_len, dbg=None):
    _, needed = _build(L, n_seq, seq_len, None)
    nc, _ = _build(L, n_seq, seq_len, needed)
    return nc


def _build(L, n_seq, seq_len, needed):
    P = 128
    dbg = None
    nc = bass.Bass("TRN2", target_bir_lowering=False)
    NTOK = n_seq * seq_len
    tiles_per_seq = seq_len // TT
    n_tiles = n_seq * tiles_per_seq

    def din(name, shape):
        return nc.dram_tensor(name, list(shape), F32, kind="ExternalInput").ap()

    x_d = din("x", [NTOK, D])
    c_d = din("c", [n_seq, D])
    ada_w = din("ada_w", [L, D, 6 * D])
    ada_b = din("ada_b", [L, 6 * D])
    pre_mix_g = din("pre_mix_g", [L, D])
    post_mix_g = din("post_mix_g", [L, D])
    w_in = din("w_in", [L, D, 3 * DC])
    conv_w = din("conv_w", [L, 31, DC])
    conv_b = din("conv_b", [L, DC])
    conv_ln_g = din("conv_ln_g", [L, DC])
    conv_ln_b = din("conv_ln_b", [L, DC])
    pool_w = din("pool_w", [L, 4, 128, 128])
    pool_scale = din("pool_scale", [L, DC])
    w_out = din("w_out", [L, D, D])
    pre_ffn_g = din("pre_ffn_g", [L, D])
    post_ffn_g = din("post_ffn_g", [L, D])
    ffn_up = din("ffn_up", [L, D, 2 * DFF])
    ffn_conv_w = din("ffn_conv_w", [L, 3, 2 * DFF])
    ffn_conv_b = din("ffn_conv_b", [L, 2 * DFF])
    ffn_down = din("ffn_down", [L, DFF, D])
    ident_d = din("ident", [P, P])
    tm_d = din("tmats", [12, P, P])
    y_d = nc.dram_tensor("y", [NTOK, D], F32, kind="ExternalOutput").ap()
    dbg_d = None
    if dbg is not None:
        dbg_d = nc.dram_tensor("dbg", [P, dbg], F32, kind="ExternalOutput").ap()

    def dscr(name, shape, dt=BF16):
        return nc.dram_tensor(name, list(shape), dt, kind="Internal").ap()

    s_in = dscr("s_in", [L, D, 3 * DC])
    s_out = dscr("s_out", [L, D, D])
    s_pool = dscr("s_pool", [L, 4, P, P])
    s_up = dscr("s_up", [L, D, 2 * DFF])
    s_down = dscr("s_down", [L, DFF, D])
    modscr = dscr("modscr", [L, 12, D], F32)

    with contextlib.ExitStack() as stack:
        em = Em(nc, stack, needed)

        xs = [em.sb(f"x{s}", [P, D], F32) for s in range(4)]
        hT = [em.sb(f"hT{k}", [P, TT], BF16) for k in range(8)]
        hid = [em.sb(f"hid{j}", [P, TT], BF16) for j in range(NF)]
        ring = [em.sb(f"ring{i}", [P, 4096], BF16) for i in range(NSLOT)]
        poolw = em.sb("poolw", [P, 4, P], BF16)
        xn = [em.sb(f"xn{i}", [P, D], BF16) for i in range(4)]
        import os
        JK = os.environ.get("K_JUNK", "0")
        junk = em.sb("junk", [P, D], BF16) if JK != "0" else None
        aT = [em.sb(f"aT{c}", [P, 30 + TT], BF16) for c in range(4)]
        ND = 32
        diag = [em.sb(f"diag{i}", [P, P], BF16) for i in range(ND)]
        cacc = [em.sb(f"cacc{c}", [P, TT], F32) for c in range(4)]
        upool = [em.sb(f"upool{s}", [P, DC], BF16) for s in range(4)]
        dT, ybf, ysq = hid[8:10], hid[10:12], hid[12:14]
        lnm = em.sb("lnm", [P, TT], F32)
        lnr = em.sb("lnr", [P, TT], F32)
        lnt = em.sb("lnt", [P, TT], F32)
        zt = [em.sb(f"zt{i}", [P, TT], F32) for i in range(2)]
        sig = zt
        mT = hid[0:8]
        ggb = [em.sb(f"ggb{g}", [P, D], F32) for g in range(2)]
        facc = cacc + zt + [lnt, lnr]
        ptmps = [em.sb(f"ptmp{i}", [P, D], F32) for i in range(2)]
        ptmp = ptmps[0]
        st4 = em.sb("st4", [P, 16], F32)
        stat = [em.sb(f"stat{i}", [P, 2], F32) for i in range(4)]
        pst = em.sb("pst", [P, 8], F32)
        fcorr = em.sb("fcorr", [P, 2 * NF, 2], F32)
        fcorr2 = em.sb("fcorr2", [P, 2 * NF], F32)
        ident_f = em.sb("ident_f", [P, P], F32)
        ident_b = em.sb("ident_b", [P, P], BF16)
        ones_b = em.sb("ones_b", [P, P], BF16)
        tm = em.sb("tm", [P, 12, P], BF16)
        stage = em.sb("stage", [P, P], F32)
        cT = em.sb("cT", [P, 16], BF16)
        colT = [em.sb(f"colT{l}", [P, 96], F32) for l in range(L)]
        cwT = [em.sb(f"cwT{l}", [P, 124], F32) for l in range(L)]
        smT = [em.sb(f"smT{l}", [P, 16], F32) for l in range(L)]
        fwT = [em.sb(f"fwT{l}", [P, 132], F32) for l in range(L)]
        fbT = [em.sb(f"fbT{l}", [P, 44], F32) for l in range(L)]
        ahalo = [[em.sb(f"ahalo{l}_{c}", [P, 30], BF16) for c in range(4)] for l in range(L)]
        phalo = [em.sb(f"phalo{l}", [P, DC], BF16) for l in range(L)]
        fhalo = [em.sb(f"fhalo{l}", [P, 2 * NF, 2], F32) for l in range(L)]
        trow, brow, grow = ptmp, ggb[0], ggb[1]
        bank = [em.ps(f"bank{i}", [P, 512], F32) for i in range(8)]
        if needed is not None:
            print("sbuf bytes remaining:", nc.sbuf_bytes_remaining)

        bank_rr = [0]
        dctr = [0]

        def nb():
            b = bank[bank_rr[0] % 8]
            bank_rr[0] += 1
            return b

        cvb = [{w: Buf(f"cv{l}_{w}", None) for w in ("in", "pool", "out", "up", "down")} for l in range(L)]
        modb = [Buf(f"modb{l}", None) for l in range(L)]
        yb = [Buf(f"yb{s}", None) for s in range(4)]

        def convert_weights(l):
            for w, dst, src in (("in", s_in[l], w_in[l]), ("pool", s_pool[l], pool_w[l]), ("out", s_out[l], w_out[l]),
                                ("up", s_up[l], ffn_up[l]), ("down", s_down[l], ffn_down[l])):
                em.dma("pool", dst, src, [], [cvb[l][w]], cvb[l][w])

        em.dma("sp", ident_f[:], ident_d, [], [ident_f], ident_f)
        em.op("dve", lambda e: e.tensor_copy(out=ident_b[:], in_=ident_f[:]), [ident_f], [ident_b])
        em.op("dve", lambda e: e.memset(ones_b[:], 1.0 / 512.0), [], [ones_b])
        for i in range(12):
            em.dma("sp", stage[:], tm_d[i], [], [stage], stage)
            em.op("dve", lambda e, i=i: e.tensor_copy(out=tm[:, i, :], in_=stage[:]), [stage], [tm])

        def transpose_rows(src_ap, nrows, dst_buf, dst_ap, evac_engine="dve", func=None):
            em.dma("sp", stage[0:nrows, :], src_ap, [], [stage], stage)
            b = nb()
            em.op("pe", lambda e: e.transpose(b[:, 0:nrows], stage[0:nrows, :], ident_f[0:nrows, 0:nrows]),
                  [stage, ident_f], [b])
            if func is None:
                em.op("dve", lambda e: e.tensor_copy(out=dst_ap, in_=b[:, 0:nrows]), [b], [dst_buf])
            else:
                em.op("act", lambda e: e.activation(out=dst_ap, in_=b[:, 0:nrows], func=func), [b], [dst_buf])

        transpose_rows(c_d.rearrange("b (k p) -> (b k) p", p=P), n_seq * 8, cT, cT[:, 0:n_seq * 8], func=AF.Silu)

        for l in range(L):
            transpose_rows(conv_w[l].rearrange("k (c p) -> (k c) p", p=P), 124, cwT[l], cwT[l][:, :])
            for i, v in enumerate((conv_b, conv_ln_g, conv_ln_b, pool_scale)):
                transpose_rows(v[l].rearrange("(c p) -> c p", p=P), 4, smT[l], smT[l][:, 4 * i:4 * i + 4])
            fw = ffn_conv_w[l].rearrange("k (j p) -> (k j) p", p=P)
            transpose_rows(fw[0:128], 128, fwT[l], fwT[l][:, 0:128])
            transpose_rows(fw[128:132], 4, fwT[l], fwT[l][:, 128:132])
            transpose_rows(ffn_conv_b[l].rearrange("(j p) -> j p", p=P), 44, fbT[l], fbT[l][:, :])

        slot_i = [0]

        def next_slot():
            s = ring[slot_i[0] % NSLOT]
            slot_i[0] += 1
            return s

        gains = {1: pre_mix_g, 2: post_mix_g, 4: pre_ffn_g, 5: post_ffn_g}
        for l in range(L):
            for v in range(6):
                halves = []
                for h in range(2):
                    sl = next_slot()
                    c0 = v * D + h * 512
                    em.dma("pool", sl[:, :].rearrange("p (k n) -> p k n", k=8),
                           ada_w[l][:, c0:c0 + 512].rearrange("(k p) n -> p k n", p=P), [], [sl], sl)
                    b = nb()
                    slv = sl[:, :].rearrange("p (k n) -> p k n", k=8)
                    cTv = cT[:, 0:n_seq * 8].rearrange("p (b k) -> p b k", k=8)
                    em.mm_group(b, b[0:n_seq, :], [(cTv[:, :, k], slv[:, k, :]) for k in range(8)], [cT, sl])
                    halves.append(b)
                em.dma("sp", brow[0:n_seq, :], ada_b[l:l + 1, v * D:(v + 1) * D].broadcast_to([n_seq, D]), [], [brow], brow)
                for h in range(2):
                    em.op("dve", lambda e, h=h: e.tensor_tensor(out=trow[0:n_seq, h * 512:(h + 1) * 512],
                                                                in0=halves[h][0:n_seq, :],
                                                                in1=brow[0:n_seq, h * 512:(h + 1) * 512], op=ALU.add),
                          [halves[h], brow], [trow])
                if v in gains:
                    em.dma("sp", grow[0:n_seq, :], gains[v][l:l + 1, :].broadcast_to([n_seq, D]), [], [grow], grow)
                    em.op("dve", lambda e: e.scalar_tensor_tensor(out=trow[0:n_seq, :], in0=trow[0:n_seq, :], scalar=1.0,
                                                                  in1=grow[0:n_seq, :], op0=ALU.add, op1=ALU.mult),
                          [trow, grow], [trow])
                em.dma("sp", modscr[l, 2 * v:2 * v + n_seq, :], trow[0:n_seq, :], [trow], [modb[l]], modb[l])
            em.dma("sp", stage[0:96, :], modscr[l].rearrange("q (c p) -> (q c) p", p=P), [modb[l]], [stage], stage)
            b = nb()
            em.op("pe", lambda e: e.transpose(b[:, 0:96], stage[0:96, :], ident_f[0:96, 0:96]), [stage, ident_f], [b])
            em.op("dve", lambda e, l=l: e.tensor_copy(out=colT[l][:, :], in_=b[:, 0:96]), [b], [colT[l]])
            if l == 0:
                convert_weights(0)


        def v8(sl, n=512):
            return sl[:, 0:8 * n].rearrange("p (k n) -> p k n", k=8)

        def v_down(sl, nj):
            return sl[:, 0:nj * D].rearrange("p (j n) -> p j n", j=nj)

        sched = []
        for i in range(n_tiles):
            for l in range(L):
                for q in range(3):
                    sched.append((cvb[l]["in"], [(lambda sl: v8(sl),
                                       s_in[l][:, q * 512:(q + 1) * 512].rearrange("(k p) n -> p k n", p=P))]))
                for q in range(2):
                    sched.append((cvb[l]["out"], [(lambda sl: v8(sl),
                                       s_out[l][:, q * 512:(q + 1) * 512].rearrange("(k p) n -> p k n", p=P))]))
                for q in range(6):
                    n = 512 if q < 5 else 256
                    for g in range(2):
                        c0 = g * DFF + q * 512
                        sched.append((cvb[l]["up"], [(lambda sl, n=n: v8(sl, n),
                                           s_up[l][:, c0:c0 + n].rearrange("(k p) n -> p k n", p=P))]))
                for q in range(6):
                    nj = 4 if q < 5 else 2
                    sched.append((cvb[l]["down"], [(lambda sl, nj=nj: v_down(sl, nj),
                                       s_down[l][q * 512:q * 512 + nj * P, :].rearrange("(j p) n -> p j n", p=P))]))
        issued = [0]
        consumed = [0]
        released = [0]
        base_slot = slot_i[0]

        def pump():
            while issued[0] < len(sched) and issued[0] < released[0] + NSLOT:
                q = issued[0]
                cv, dmas = sched[q]
                sl = ring[(base_slot + q) % NSLOT]
                for dst_fn, src in dmas:
                    em.dma("sp", dst_fn(sl), src, [cv], [sl], sl)
                issued[0] += 1

        def take():
            p = consumed[0]
            pump()
            assert issued[0] > p, "ring too small for the pieces held at once"
            consumed[0] += 1
            return ring[(base_slot + p) % NSLOT]

        def release(n):
            released[0] += n
            assert released[0] <= consumed[0]
            pump()

        def pre_elem(s):
            xb = xn[s]
            jb = junk if JK in ("1", "3") else xb
            em.op("act", lambda e: e.activation(out=jb[:, :], in_=xs[s][:, :], func=AF.Square, accum_out=pst[:, s:s + 1]),
                  [xs[s]], [jb, pst])
            em.op("act", lambda e: e.activation(out=pst[:, 4 + s:5 + s], in_=pst[:, s:s + 1], func=AF.Sqrt, scale=1.0 / D, bias=EPS),
                  [pst], [pst])
            em.op("dve", lambda e: e.reciprocal(out=pst[:, 4 + s:5 + s], in_=pst[:, 4 + s:5 + s]), [pst], [pst])
            em.op("dve", lambda e: e.tensor_scalar(out=xb[:, :], in0=xs[s][:, :], scalar1=pst[:, 4 + s:5 + s], scalar2=None,
                                                   op0=ALU.mult), [xs[s], pst], [xb])

        def pre_finish(l, seq, vsc, vsh):
            ca = (vsc * 2 + seq) * 8
            cb = (vsh * 2 + seq) * 8
            tb = [nb(), nb(), nb(), nb()]
            for s in range(4):
                xb = xn[s]
                fns = []
                for k in range(8):
                    bv = tb[k // 2][:, :].bitcast(BF16)
                    o = bv[:, (k % 2) * 512 + s * 128:(k % 2) * 512 + (s + 1) * 128]
                    fns.append(lambda e, o=o, xb=xb, k=k: e.transpose(o, xb[:, k * 128:(k + 1) * 128], ident_b[:, :]))
                em.multi("pe", fns, [xb, ident_b], tb)
            for k in range(8):
                b = tb[k // 2]
                bv = b[:, :].bitcast(BF16)
                src = bv[:, (k % 2) * 512:(k % 2) * 512 + 512]
                if k % 2 == 0:
                    em.op("act", lambda e, k=k, src=src: e.activation(out=hT[k][:, :], in_=src, func=AF.Identity,
                                                                     scale=colT[l][:, ca + k:ca + k + 1],
                                                                     bias=colT[l][:, cb + k:cb + k + 1]), [b, colT[l]], [hT[k]])
                else:
                    em.op("dve", lambda e, k=k, src=src: e.tensor_scalar(out=hT[k][:, :], in0=src, scalar1=colT[l][:, ca + k:ca + k + 1],
                                                                        scalar2=colT[l][:, cb + k:cb + k + 1], op0=ALU.mult,
                                                                        op1=ALU.add), [b, colT[l]], [hT[k]])

        def postnorm(vg, s, ob, final=None, pool_add=False):
            st = stat[s]
            g = ggb[0 if vg == 2 else 1]
            for h in range(2):
                jb2 = junk if JK in ("2", "3") else ptmp
                em.op("act", lambda e, h=h: e.activation(out=jb2[:, h * 512:(h + 1) * 512], in_=ob[h][:, :], func=AF.Square,
                                                         accum_out=st[:, h:h + 1]), [ob[h]], [jb2, st])
            em.op("dve", lambda e: e.tensor_tensor(out=st[:, 0:1], in0=st[:, 0:1], in1=st[:, 1:2], op=ALU.add), [st], [st])
            em.op("act", lambda e: e.activation(out=st[:, 1:2], in_=st[:, 0:1], func=AF.Sqrt, scale=1.0 / D, bias=EPS), [st], [st])
            em.op("dve", lambda e: e.reciprocal(out=st[:, 1:2], in_=st[:, 1:2]), [st], [st])
            for h in range(2):
                em.op("dve", lambda e, h=h: e.scalar_tensor_tensor(out=ptmp[:, h * 512:(h + 1) * 512], in0=ob[h][:, :],
                                                                   scalar=st[:, 1:2], in1=g[:, h * 512:(h + 1) * 512],
                                                                   op0=ALU.mult, op1=ALU.mult), [ob[h], st, g], [ptmp])
            if final is None:
                em.op("pool" if pool_add else "dve",
                      lambda e: e.tensor_tensor(out=xs[s][:, :], in0=xs[s][:, :], in1=ptmp[:, :], op=ALU.add),
                      [xs[s], ptmp], [xs[s]])
            else:
                r0, nr0 = final
                em.op("dve", lambda e: e.tensor_tensor(out=ptmp[:, :], in0=xs[s][:, :], in1=ptmp[:, :], op=ALU.add),
                      [xs[s], ptmp], [ptmp])
                em.dma("sp", y_d[r0:r0 + 128, :], ptmp[:, :], [ptmp], [yb[s]], yb[s])
                if nr0 is not None:
                    em.dma("sp", xs[s][:, :], x_d[nr0:nr0 + 128, :], [], [xs[s]], xs[s])

        def load_gg(l, seq):
            for gi, v in enumerate((2, 5)):
                em.dma("sp", ggb[gi][:, :], modscr[l, 2 * v + seq:2 * v + seq + 1, :].broadcast_to([P, D]),
                       [modb[l]], [ggb[gi]], ggb[gi])

        def mixer(i, l):
            seq = i // tiles_per_seq
            first = (i % tiles_per_seq) == 0
            dbase = dctr[0]

            def build_diag(c, j):
                dg = diag[(dbase + c * 31 + j) % ND]
                w = cwT[l][:, j * 4 + c:j * 4 + c + 1]
                if j % 2 == 0:
                    em.op("dve", lambda e: e.tensor_scalar(out=dg[:, :], in0=ident_b[:, :], scalar1=w, scalar2=None, op0=ALU.mult),
                          [ident_b, cwT[l]], [dg])
                else:
                    em.op("act", lambda e: e.activation(out=dg[:, :], in_=ident_b[:, :], func=AF.Copy, scale=w),
                          [ident_b, cwT[l]], [dg])

            pre_finish(l, seq, 1, 0)
            w_val, w_gate, w_pool = take(), take(), take()
            for c in range(4):
                if first:
                    em.op("pool", lambda e, c=c: e.memset(aT[c][:, 0:30], 0.0), [], [aT[c]])
                else:
                    em.op("pool", lambda e, c=c: e.tensor_copy(out=aT[c][:, 0:30], in_=ahalo[l][c][:, :]),
                          [ahalo[l][c]], [aT[c]])
                bv, bg = nb(), nb()
                if c == 0:
                    for k in range(8):
                        em.mm_group(bv, bv[:, :], [(v8(w_val)[:, k, 0:128], hT[k][:, :])], [w_val, hT[k]],
                                    first=(k == 0), last=(k == 7))
                else:
                    em.mm_group(bv, bv[:, :], [(v8(w_val)[:, k, c * 128:(c + 1) * 128], hT[k][:, :]) for k in range(8)],
                                [w_val] + hT)
                em.mm_group(bg, bg[:, :], [(v8(w_gate)[:, k, c * 128:(c + 1) * 128], hT[k][:, :]) for k in range(8)],
                            [w_gate] + hT)
                sg = sig[c % 2]
                em.op("act", lambda e, sg=sg, bg=bg: e.activation(out=sg[:, :], in_=bg[:, :], func=AF.Sigmoid), [bg], [sg])
                em.op("dve", lambda e, c=c, sg=sg, bv=bv: e.tensor_tensor(out=aT[c][:, 30:30 + TT], in0=bv[:, :], in1=sg[:, :],
                                                                          op=ALU.mult), [bv, sg], [aT[c]])
                em.op("pool", lambda e, c=c: e.tensor_copy(out=ahalo[l][c][:, :], in_=aT[c][:, TT:TT + 30]),
                      [aT[c]], [ahalo[l][c]])
            release(2)
            for s in range(4):
                b = nb()
                em.mm_group(b, b[:, :], [(hT[k][:, s * 128:(s + 1) * 128], v8(w_pool)[:, k, :]) for k in range(8)],
                            [w_pool] + hT)
                em.op("act", lambda e, s=s, b=b: e.activation(out=upool[s][:, :], in_=b[:, :], func=AF.Copy), [b], [upool[s]])
            release(1)
            for j in range(31):
                build_diag(0, j)
            lnb = []

            def ln_act(c):
                yb_, yq_ = ybf[c % 2], ysq[c % 2]
                em.op("act", lambda e: e.activation(out=yb_[:, :], in_=cacc[c][:, :], func=AF.Copy), [cacc[c]], [yb_])
                em.op("act", lambda e: e.activation(out=yq_[:, :], in_=cacc[c][:, :], func=AF.Square), [cacc[c]], [yq_])

            def ln_mm(c):
                if not lnb:
                    lnb.extend([nb(), nb()])
                bm, bq = lnb
                yb_, yq_ = ybf[c % 2], ysq[c % 2]
                need = em._deps("pe", [ones_b, yb_, yq_], [bm, bq])
                em._wait("pe", need)
                nc.tensor.matmul(bm[:, :], ones_b[:, :], yb_[:, :], start=(c == 0), stop=(c == 3))
                i2 = nc.tensor.matmul(bq[:, :], ones_b[:, :], yq_[:, :], start=(c == 0), stop=(c == 3))
                em._inc("pe", i2)
                em.ninst += 2
                dep = ("pe", em.sem["pe"], em.cnt["pe"], "pe")
                em._mark(dep, [ones_b, yb_, yq_], [bm, bq])

            for c in range(4):
                b = nb()
                for j0 in range(0, 31, 4):
                    js = list(range(j0, min(j0 + 4, 31)))
                    dgs = [diag[(dbase + c * 31 + j) % ND] for j in js]
                    em.mm_group(b, b[:, :], [(dg[:, :], aT[c][:, j:j + TT]) for dg, j in zip(dgs, js)], dgs + [aT[c]],
                                first=(j0 == 0), last=(js[-1] == 30))
                    if c < 3:
                        for j in js:
                            build_diag(c + 1, j)
                if c >= 1:
                    ln_mm(c - 1)
                em.op("act", lambda e, c=c, b=b: e.activation(out=cacc[c][:, :], in_=b[:, :], func=AF.Identity,
                                                             bias=smT[l][:, c:c + 1]), [b, smT[l]], [cacc[c]])
                ln_act(c)
            dctr[0] += 124

            def pool_T(g):
                b = nb()
                for s in range(4):
                    cur = upool[s][:, g * 128:(g + 1) * 128]
                    o = b[:, s * 128:(s + 1) * 128]
                    if s == 0 and first:
                        em.mm_group(b, o, [(cur, tm[:, 8 + g, :])], [upool[s], tm])
                    else:
                        pb = phalo[l] if s == 0 else upool[s - 1]
                        em.mm_group(b, o, [(cur, tm[:, g, :]), (pb[:, g * 128:(g + 1) * 128], tm[:, 4 + g, :])],
                                    [upool[s], pb, tm])
                dd = dT[g % 2]
                em.op("dve", lambda e: e.tensor_copy(out=dd[:, :], in_=b[:, :]), [b], [dd])

            def pool_W(g):
                dd = dT[g % 2]
                b2 = nb()
                em.mm_group(b2, b2[:, :], [(poolw[:, g, :], dd[:, :])], [poolw, dd])
                em.op("dve", lambda e: e.tensor_scalar(out=mT[4 + g][:, :], in0=b2[:, :], scalar1=smT[l][:, 12 + g:13 + g],
                                                       scalar2=None, op0=ALU.mult), [b2, smT[l]], [mT[4 + g]])

            em.dma("sp", poolw[:, :, :], s_pool[l].rearrange("g c d -> c g d"), [cvb[l]["pool"]], [poolw], poolw)
            pool_T(0)
            ln_mm(3)
            bm, bq = lnb
            em.op("act", lambda e: e.activation(out=lnm[:, :], in_=bm[:, :], func=AF.Copy), [bm], [lnm])
            em.op("dve", lambda e: e.tensor_tensor(out=lnt[:, :], in0=lnm[:, :], in1=lnm[:, :], op=ALU.mult), [lnm], [lnt])
            em.op("dve", lambda e: e.tensor_tensor(out=lnt[:, :], in0=bq[:, :], in1=lnt[:, :], op=ALU.subtract), [bq, lnt], [lnt])
            pool_T(1)
            pool_W(0)
            pool_T(2)
            pool_W(1)
            pool_T(3)
            pool_W(2)
            pool_W(3)
            em.op("pool", lambda e: e.tensor_copy(out=phalo[l][:, :], in_=upool[3][:, :]), [upool[3]], [phalo[l]])
            wo = [take(), take()]
            load_gg(l, seq)
            obs = []
            for s in range(4):
                ob = [nb(), nb()]
                for h in range(2):
                    em.mm_group(ob[h], ob[h][:, :],
                                [(mT[k][:, s * 128:(s + 1) * 128], v8(wo[h])[:, k, :]) for k in range(4, 8)], [wo[h]] + mT[4:8],
                                first=True, last=False)
                obs.append(ob)
            wo_banks = (wo, obs)
            em.op("act", lambda e: e.activation(out=lnr[:, :], in_=lnt[:, :], func=AF.Sqrt, bias=EPS), [lnt], [lnr])
            em.op("dve", lambda e: e.reciprocal(out=lnr[:, :], in_=lnr[:, :]), [lnr], [lnr])
            for c in range(4):
                z = zt[c % 2]
                em.op("dve", lambda e, c=c, z=z: e.tensor_tensor(out=z[:, :], in0=cacc[c][:, :], in1=lnm[:, :], op=ALU.subtract),
                      [cacc[c], lnm], [z])
                em.op("dve", lambda e, z=z: e.tensor_tensor(out=z[:, :], in0=z[:, :], in1=lnr[:, :], op=ALU.mult), [z, lnr], [z])
                em.op("act", lambda e, c=c, z=z: e.activation(out=mT[c][:, :], in_=z[:, :], func=AF.Silu,
                                                             scale=smT[l][:, 4 + c:5 + c], bias=smT[l][:, 8 + c:9 + c]),
                      [z, smT[l]], [mT[c]])
            return wo_banks

        def mixer_tail(i, l, wo, obs):
            g = ggb[0]
            for s in range(4):
                ob = obs[s]
                for h in range(2):
                    em.mm_group(ob[h], ob[h][:, :],
                                [(mT[k][:, s * 128:(s + 1) * 128], v8(wo[h])[:, k, :]) for k in range(4)], [wo[h]] + mT[0:4],
                                first=False, last=True)
            release(2)
            for s in range(4):
                for h in range(2):
                    em.op("act", lambda e, s=s, h=h: e.activation(out=xn[s][:, h * 512:(h + 1) * 512], in_=obs[s][h][:, :],
                                                                 func=AF.Square, accum_out=st4[:, 2 * s + h:2 * s + h + 1]),
                          [obs[s][h]], [xn[s], st4])
            em.op("dve", lambda e: e.tensor_tensor(out=st4[:, 8:12], in0=st4[:, 0:8:2], in1=st4[:, 1:8:2], op=ALU.add), [st4], [st4])
            em.op("act", lambda e: e.activation(out=st4[:, 12:16], in_=st4[:, 8:12], func=AF.Sqrt, scale=1.0 / D, bias=EPS),
                  [st4], [st4])
            em.op("dve", lambda e: e.reciprocal(out=st4[:, 12:16], in_=st4[:, 12:16]), [st4], [st4])
            for s in range(4):
                pt = ptmps[s % 2]
                for h in range(2):
                    em.op("dve", lambda e, s=s, h=h, pt=pt: e.scalar_tensor_tensor(
                        out=pt[:, h * 512:(h + 1) * 512], in0=obs[s][h][:, :], scalar=st4[:, 12 + s:13 + s],
                        in1=g[:, h * 512:(h + 1) * 512], op0=ALU.mult, op1=ALU.mult), [obs[s][h], st4, g], [pt])
                em.op("pool" if s < 3 else "dve",
                      lambda e, s=s, pt=pt: e.tensor_tensor(out=xs[s][:, :], in0=xs[s][:, :], in1=pt[:, :], op=ALU.add),
                      [xs[s], pt], [xs[s]])
            for s in range(4):
                pre_elem(s)

        def ffn(i, l):
            seq = i // tiles_per_seq
            first = (i % tiles_per_seq) == 0
            if i == 0 and l + 1 < L:
                convert_weights(l + 1)
            pre_finish(l, seq, 4, 3)
            if first:
                em.op("pool", lambda e: e.memset(fhalo[l][:, :, :], 0.0), [], [fhalo[l]])
            fh = fhalo[l]
            W0, W1 = fwT[l][:, 0:44], fwT[l][:, 44:88]
            em.op("pool", lambda e: e.tensor_tensor(out=fcorr[:, :, 0], in0=fh[:, :, 0], in1=W0, op=ALU.mult), [fh, fwT[l]], [fcorr])
            em.op("pool", lambda e: e.tensor_tensor(out=fcorr[:, :, 1], in0=fh[:, :, 1], in1=W0, op=ALU.mult), [fh, fwT[l]], [fcorr])
            em.op("pool", lambda e: e.tensor_tensor(out=fcorr2[:, :], in0=fh[:, :, 1], in1=W1, op=ALU.mult), [fh, fwT[l]], [fcorr2])
            em.op("pool", lambda e: e.tensor_tensor(out=fcorr[:, :, 0], in0=fcorr[:, :, 0], in1=fcorr2[:, :], op=ALU.add),
                  [fcorr, fcorr2], [fcorr])
            pieces = {}

            def stage_ab(j):
                q, jj = j // 4, j % 4
                n = 512 if q < 5 else 256
                if jj == 0:
                    pieces[q] = (take(), take())
                banks = []
                for gi, w in enumerate(pieces[q]):
                    b = nb()
                    if j == 0 and gi == 0:
                        for k in range(8):
                            em.mm_group(b, b[:, :], [(v8(w, n)[:, k, 0:128], hT[k][:, :])], [w, hT[k]],
                                        first=(k == 0), last=(k == 7))
                    else:
                        em.mm_group(b, b[:, :], [(v8(w, n)[:, k, jj * 128:(jj + 1) * 128], hT[k][:, :]) for k in range(8)],
                                    [w] + hT)
                    banks.append(b)
                if jj == n // 128 - 1:
                    release(2)
                for gi, b in enumerate(banks):
                    ch = gi * NF + j
                    acc = facc[(j % 4) * 2 + gi]
                    w2 = fwT[l][:, 2 * 44 + ch:2 * 44 + ch + 1]
                    em.op("act", lambda e, acc=acc, b=b, w2=w2, ch=ch: e.activation(out=acc[:, :], in_=b[:, :], func=AF.Identity,
                                                                                   scale=w2, bias=fbT[l][:, ch:ch + 1]),
                          [b, fwT[l], fbT[l]], [acc])
                    em.op("act", lambda e, ch=ch, b=b: e.activation(out=fh[:, ch, 0:2], in_=b[:, TT - 2:TT], func=AF.Copy), [b], [fh])
                for gi, b in enumerate(banks):
                    ch = gi * NF + j
                    acc = facc[(j % 4) * 2 + gi]
                    w0 = fwT[l][:, 0 * 44 + ch:0 * 44 + ch + 1]
                    w1 = fwT[l][:, 1 * 44 + ch:1 * 44 + ch + 1]
                    em.op("dve", lambda e, acc=acc, b=b, w1=w1: e.scalar_tensor_tensor(out=acc[:, 1:TT], in0=b[:, 0:TT - 1], scalar=w1,
                                                                                      in1=acc[:, 1:TT], op0=ALU.mult, op1=ALU.add),
                          [b, fwT[l], acc], [acc])
                    em.op("dve", lambda e, acc=acc, b=b, w0=w0: e.scalar_tensor_tensor(out=acc[:, 2:TT], in0=b[:, 0:TT - 2], scalar=w0,
                                                                                      in1=acc[:, 2:TT], op0=ALU.mult, op1=ALU.add),
                          [b, fwT[l], acc], [acc])
                for gi in range(2):
                    ch = gi * NF + j
                    acc = facc[(j % 4) * 2 + gi]
                    em.op("pool", lambda e, acc=acc, ch=ch: e.tensor_tensor(out=acc[:, 0:2], in0=acc[:, 0:2], in1=fcorr[:, ch, :],
                                                                            op=ALU.add), [acc, fcorr], [acc])

            def stage_d(j):
                av, ag = facc[(j % 4) * 2], facc[(j % 4) * 2 + 1]
                em.op("act", lambda e: e.activation(out=ag[:, :], in_=ag[:, :], func=AF.Silu), [ag], [ag])
                em.op("pool", lambda e: e.tensor_tensor(out=hid[j][:, :], in0=av[:, :], in1=ag[:, :], op=ALU.mult),
                      [av, ag], [hid[j]])

            for j in range(NF + 2):
                if j < NF:
                    stage_ab(j)
                if j >= 2:
                    stage_d(j - 2)
            wd = [take() for _ in range(6)]
            for s in range(4):
                ob = [nb(), nb()]
                for h in range(2):
                    pairs = []
                    for j in range(NF):
                        q, jj = j // 4, j % 4
                        nj = 4 if q < 5 else 2
                        pairs.append((hid[j][:, s * 128:(s + 1) * 128], v_down(wd[q], nj)[:, jj, h * 512:(h + 1) * 512]))
                    if s == 0 and h == 0:
                        em.mm_group(ob[h], ob[h][:, :], pairs[:16], wd + hid[:16], first=True, last=False)
                        em.mm_group(ob[h], ob[h][:, :], pairs[16:], wd + hid[16:], first=False, last=True)
                    else:
                        em.mm_group(ob[h], ob[h][:, :], pairs, wd + hid)
                if l < L - 1:
                    postnorm(5, s, ob, pool_add=(s < 3))
                    pre_elem(s)
                else:
                    r0 = i * TT + s * 128
                    postnorm(5, s, ob, final=(r0, r0 + TT if i + 1 < n_tiles else None))
                    if i + 1 < n_tiles:
                        pre_elem(s)
            release(6)

        for s in range(4):
            em.dma("sp", xs[s][:, :], x_d[s * 128:(s + 1) * 128, :], [], [xs[s]], xs[s])
            pre_elem(s)
        for i in range(n_tiles):
            for l in range(L):
                wo, obs = mixer(i, l)
                mixer_tail(i, l, wo, obs)
                ffn(i, l)
        for s in range(4):
            nc.sync.wait_ge(yb[s].dsem, yb[s].dcnt)
        if needed is not None:
            print("instructions:", em.ninst, "waits:", em.nwaits, "milestones:", em.cnt, "incs:", em.real)
    return nc, em.waited


def _consts():
    ident = np.eye(128, dtype=np.float32)
    tm = np.zeros((12, 128, 128), np.float32)
    tp = np.arange(128)[:, None]
    t = np.arange(128)[None, :]
    for g, w in enumerate(POOL_W):
        d = t - tp
        tm[g] = ((d >= 0) & (d < w)) / w - (d == 0)
        d2 = t + 128 - tp
        tm[4 + g] = ((d2 > 0) & (d2 < w)) / w
        cnt = np.minimum(t + 1, w).astype(np.float32)
        tm[8 + g] = ((d >= 0) & (d < w)) / cnt - (d == 0)
    return ident, tm


_NAMES = ["ada_w", "ada_b", "pre_mix_g", "post_mix_g", "w_in", "conv_w", "conv_b", "conv_ln_g", "conv_ln_b",
          "pool_w", "pool_scale", "w_out", "pre_ffn_g", "post_ffn_g", "ffn_up", "ffn_conv_w", "ffn_conv_b", "ffn_down"]


def run(inputs, n_cores, L, seq_per_core, seq_len, dbg=None):
    nc = build(L, seq_per_core, seq_len, dbg=dbg)
    ident, tm = _consts()
    x = np.ascontiguousarray(inputs["x"], dtype=np.float32)
    c = np.ascontiguousarray(inputs["c"], dtype=np.float32)
    in_maps = []
    for r in range(n_cores):
        m = {"x": x[r * seq_per_core:(r + 1) * seq_per_core].reshape(seq_per_core * seq_len, D),
             "c": c[r * seq_per_core:(r + 1) * seq_per_core], "ident": ident, "tmats": tm}
        for n in _NAMES:
            m[n] = np.ascontiguousarray(inputs[n][:L], dtype=np.float32)
        in_maps.append(m)
    res = run_bass_kernel_spmd(nc, in_maps, core_ids=list(range(n_cores)))
    y = np.concatenate([r["y"].reshape(seq_per_core, seq_len, D) for r in res.results], axis=0)
    if dbg is not None:
        return y, [r["dbg"] for r in res.results]
    return y


def kernel(**inputs):
    return run(inputs, 8, 4, 2, 2048).astype(np.float32)
```
